# Optimizing a Trainium2 kernel written in Bass

```python
import math
import jax, jax.numpy as jnp
from jax import lax
import numpy as np

D_MODEL = 2048
BATCH = 8
SEQ = 4096
DEPTH = 4

HEAD_DIM = 64
N_HEADS_TOTAL = D_MODEL // HEAD_DIM
N_HEADS_A = N_HEADS_TOTAL // 4
N_KV_A = N_HEADS_A // 4
N_HEADS_B = N_HEADS_TOTAL // 4
N_HEADS_C = N_HEADS_TOTAL // 2
BLOCK = 128
WINDOW_A = 128
DILATED_PAIRS = ((128, 1), (512, 4), (2048, 16))
N_BUCKETS = 32
T5_MAX_DIST = 2048
D_FF = 256 * (-(-(8 * D_MODEL // 3) // 256))
CONV_WIDTH = 3
EPS = 1e-6
NEG_INF = -1e30

A_Q = N_HEADS_A * HEAD_DIM
A_KV = N_KV_A * HEAD_DIM
B_W = N_HEADS_B * HEAD_DIM
C_W = N_HEADS_C * HEAD_DIM
IN_WIDTH = A_Q + 2 * A_KV + 3 * B_W + 3 * C_W
MIX_WIDTH = A_Q + B_W + C_W

kernel_name = "hymba_style_swa_stickbreak_dilated_convffn"


def rmsnorm(x, g):
    xf = x.astype(jnp.float32)
    y = xf * lax.rsqrt(jnp.mean(xf * xf, axis=-1, keepdims=True) + EPS)
    return (y * g.astype(jnp.float32)).astype(x.dtype)


def t5_bucket(dist):
    max_exact = N_BUCKETS // 2
    d = jnp.maximum(dist, 0)
    large = max_exact + (jnp.log(jnp.maximum(d, 1).astype(jnp.float32) / max_exact)
                         / math.log(T5_MAX_DIST / max_exact) * (N_BUCKETS - max_exact)).astype(jnp.int32)
    large = jnp.minimum(large, N_BUCKETS - 1)
    return jnp.where(d < max_exact, d, large)


def block_rel_bias(table, dil):
    rel = jnp.arange(BLOCK)[:, None] + BLOCK - jnp.arange(2 * BLOCK)[None, :]
    buckets = t5_bucket(rel * dil)
    return jnp.transpose(table[buckets], (2, 0, 1)).astype(jnp.float32)


def banded_attention(q, k, v, bias, max_dist, sinks=None):
    n, length, hq, hd = q.shape
    hk = k.shape[2]
    grp = hq // hk
    lp = -(-length // BLOCK) * BLOCK
    if lp != length:
        padw = ((0, 0), (0, lp - length), (0, 0), (0, 0))
        q, k, v = jnp.pad(q, padw), jnp.pad(k, padw), jnp.pad(v, padw)
    nb = lp // BLOCK
    qb = q.reshape(n, nb, BLOCK, hk, grp, hd).astype(jnp.float32)
    kb = k.reshape(n, nb, BLOCK, hk, hd).astype(jnp.float32)
    vb = v.reshape(n, nb, BLOCK, hk, hd).astype(jnp.float32)
    prev = ((0, 0), (1, 0), (0, 0), (0, 0), (0, 0))
    kk = jnp.concatenate([jnp.pad(kb[:, :-1], prev), kb], axis=2)
    vv = jnp.concatenate([jnp.pad(vb[:, :-1], prev), vb], axis=2)
    logits = (jnp.einsum('nbqhgd,nbkhd->nbhgqk', qb, kk) * (hd ** -0.5)
              + bias.reshape(hk, grp, BLOCK, 2 * BLOCK))
    rel = jnp.arange(BLOCK)[:, None] + BLOCK - jnp.arange(2 * BLOCK)[None, :]
    key_abs = (jnp.arange(nb)[:, None] - 1) * BLOCK + jnp.arange(2 * BLOCK)[None, :]
    mask = ((rel >= 0) & (rel <= max_dist))[None] & (key_abs >= 0)[:, None, :]
    logits = jnp.where(mask[None, :, None, None], logits, NEG_INF)
    m = jnp.max(logits, axis=-1, keepdims=True)
    if sinks is not None:
        s = sinks.astype(jnp.float32).reshape(hk, grp, 1, 1)
        m = jnp.maximum(m, s)
    p = jnp.exp(logits - m)
    denom = jnp.sum(p, axis=-1, keepdims=True)
    if sinks is not None:
        denom = denom + jnp.exp(s - m)
    out = jnp.einsum('nbhgqk,nbkhd->nbhgqd', p / denom, vv)
    out = jnp.transpose(out, (0, 1, 4, 2, 3, 5)).reshape(n, lp, hq, hd)[:, :length]
    lse = jnp.transpose((m + jnp.log(denom))[..., 0], (0, 1, 4, 2, 3)).reshape(n, lp, hq)[:, :length]
    return out, lse


def stick_breaking_attention(q, k, v):
    b_, s_, h, hd = q.shape
    nb = s_ // BLOCK
    qb = jnp.transpose(q.reshape(b_, nb, BLOCK, h, hd), (1, 0, 2, 3, 4))
    kf = k.astype(jnp.float32)
    vf = v.astype(jnp.float32)
    s_pos = jnp.arange(s_)

    def one_block(args):
        qblk, blk = args
        z = jnp.einsum('bqhd,bkhd->bhqk', qblk.astype(jnp.float32), kf) * (hd ** -0.5)
        t_pos = blk * BLOCK + jnp.arange(BLOCK)
        causal = s_pos[None, :] < t_pos[:, None]
        log_rem = jnp.where(causal, jax.nn.log_sigmoid(-z), 0.0)
        suffix = lax.cumsum(log_rem, axis=3, reverse=True) - log_rem
        a = jnp.where(causal, jnp.exp(jax.nn.log_sigmoid(z) + suffix), 0.0)
        return jnp.einsum('bhqk,bkhd->bqhd', a, vf)

    out = lax.map(one_block, (qb, jnp.arange(nb)))
    return jnp.transpose(out, (1, 0, 2, 3, 4)).reshape(b_, s_, h, hd)


def dilated_attention(q, k, v, table_c):
    b_, s_, h, hd = q.shape
    outs, lses = [], []
    for window, dil in DILATED_PAIRS:
        def to_sub(t):
            return jnp.transpose(t.reshape(b_, s_ // dil, dil, h, hd), (0, 2, 1, 3, 4)).reshape(b_ * dil, s_ // dil, h, hd)
        o, lse = banded_attention(to_sub(q), to_sub(k), to_sub(v), block_rel_bias(table_c, dil), window // dil)
        outs.append(jnp.transpose(o.reshape(b_, dil, s_ // dil, h, hd), (0, 2, 1, 3, 4)).reshape(b_, s_, h, hd))
        lses.append(jnp.transpose(lse.reshape(b_, dil, s_ // dil, h), (0, 2, 1, 3)).reshape(b_, s_, h))
    w = jax.nn.softmax(jnp.stack(lses, axis=0), axis=0)
    return jnp.sum(w[..., None] * jnp.stack(outs, axis=0), axis=0)


def causal_dwconv(u, w, b):
    up = jnp.pad(u, ((0, 0), (CONV_WIDTH - 1, 0), (0, 0)))
    s_ = u.shape[1]
    acc = b
    for i in range(CONV_WIDTH):
        acc = acc + w[i] * up[:, i:i + s_]
    return acc


def setup_inputs(seed: int = 0) -> dict:
    key = jax.random.key(seed)
    ks = jax.random.split(key, 20)
    f32 = jnp.float32

    def nrm(k, shape, scale):
        return jax.random.normal(k, shape, f32) * scale

    return {
        "x": nrm(ks[0], (BATCH, SEQ, D_MODEL), 1.0),
        "attn_norm": 1.0 + nrm(ks[1], (DEPTH, D_MODEL), 0.02),
        "w_in": nrm(ks[2], (DEPTH, D_MODEL, IN_WIDTH), D_MODEL ** -0.5),
        "a_q_gain": 1.0 + nrm(ks[3], (DEPTH, HEAD_DIM), 0.02),
        "a_k_gain": 1.0 + nrm(ks[4], (DEPTH, HEAD_DIM), 0.02),
        "a_sinks": nrm(ks[5], (DEPTH, N_HEADS_A), 0.5),
        "c_q_gain": 1.0 + nrm(ks[6], (DEPTH, HEAD_DIM), 0.02),
        "c_k_gain": 1.0 + nrm(ks[7], (DEPTH, HEAD_DIM), 0.02),
        "rel_bias_table": nrm(ks[8], (N_BUCKETS, N_HEADS_A + N_HEADS_C), 0.5),
        "mix_out_gain": 1.0 + nrm(ks[9], (DEPTH, MIX_WIDTH), 0.02),
        "w_out": nrm(ks[10], (DEPTH, MIX_WIDTH, D_MODEL), MIX_WIDTH ** -0.5),
        "ffn_norm": 1.0 + nrm(ks[11], (DEPTH, D_MODEL), 0.02),
        "w_up": nrm(ks[12], (DEPTH, D_MODEL, 2 * D_FF), D_MODEL ** -0.5),
        "conv_w": nrm(ks[13], (DEPTH, CONV_WIDTH, 2 * D_FF), CONV_WIDTH ** -0.5),
        "conv_b": nrm(ks[14], (DEPTH, 2 * D_FF), 0.02),
        "w_down": nrm(ks[15], (DEPTH, D_FF, D_MODEL), D_FF ** -0.5),
    }


def reference(x, attn_norm, w_in, a_q_gain, a_k_gain, a_sinks, c_q_gain, c_k_gain, rel_bias_table,
              mix_out_gain, w_out, ffn_norm, w_up, conv_w, conv_b, w_down):
    b_, s_, _ = x.shape
    sizes = [A_Q, A_KV, A_KV, B_W, B_W, B_W, C_W, C_W, C_W]
    offsets = [int(o) for o in np.cumsum(sizes)[:-1]]
    table_a = rel_bias_table[:, :N_HEADS_A]
    table_c = rel_bias_table[:, N_HEADS_A:]
    bias_a = block_rel_bias(table_a, 1)
    for l in range(DEPTH):
        h = rmsnorm(x, attn_norm[l])
        proj = jnp.einsum('bsd,de->bse', h, w_in[l])
        aq, ak, av, bq, bk, bv, cq, ck, cv = jnp.split(proj, offsets, axis=-1)
        heads = lambda t, n: t.reshape(b_, s_, n, HEAD_DIM)
        out_a, _ = banded_attention(rmsnorm(heads(aq, N_HEADS_A), a_q_gain[l]),
                                    rmsnorm(heads(ak, N_KV_A), a_k_gain[l]),
                                    heads(av, N_KV_A), bias_a, WINDOW_A - 1, sinks=a_sinks[l])
        out_b = stick_breaking_attention(heads(bq, N_HEADS_B), heads(bk, N_HEADS_B), heads(bv, N_HEADS_B))
        out_c = dilated_attention(rmsnorm(heads(cq, N_HEADS_C), c_q_gain[l]),
                                  rmsnorm(heads(ck, N_HEADS_C), c_k_gain[l]),
                                  heads(cv, N_HEADS_C), table_c)
        g = mix_out_gain[l]
        ya = rmsnorm(out_a.reshape(b_, s_, A_Q), g[:A_Q])
        yb = rmsnorm(out_b.reshape(b_, s_, B_W), g[A_Q:A_Q + B_W])
        yc = rmsnorm(out_c.reshape(b_, s_, C_W), g[A_Q + B_W:])
        mix = jnp.concatenate([ya, yb, yc], axis=-1).astype(x.dtype)
        x = x + jnp.einsum('bse,ed->bsd', mix, w_out[l])
        h = rmsnorm(x, ffn_norm[l])
        u = causal_dwconv(jnp.einsum('bsd,df->bsf', h, w_up[l]), conv_w[l], conv_b[l])
        gate, up = jnp.split(u, [D_FF], axis=-1)
        x = x + jnp.einsum('bsf,fd->bsd', jax.nn.silu(gate) * up, w_down[l])
    return x
```

```python
import math
from contextlib import ExitStack

import numpy as np
import concourse.bass as bass
import concourse.mybir as mybir
from concourse.bass_utils import run_bass_kernel_spmd

F32 = mybir.dt.float32
BF16 = mybir.dt.bfloat16
AF = mybir.ActivationFunctionType
ALU = mybir.AluOpType

D = 2048
S = 4096
KC = D // 128
HD = 64
IN_W = 5376
DFF = 5632
FC = DFF // 128
EPS = 1e-6
N_BUCKETS = 32
T5_MAX = 2048
DILS = (1, 4, 16)
NVCOL = 1664

ENGS = ["pe", "act", "dve", "pool", "sp"]
N_DMA_SEMS = 6


class Sched:
    def __init__(self, nc):
        self.nc = nc
        self.ops = {e: [] for e in ENGS}
        self.cnt = {}
        self.waited = {e: {} for e in ENGS}
        self.last_w = {}
        self.readers = {}
        self.sems = {}
        self.dma_rr = {"sp": 0, "pool": 0, "act": 0}
        self.n_instr = 0

    def sem_names(self):
        names = list(ENGS)
        for q in ("sp", "pool", "act"):
            for i in range(N_DMA_SEMS):
                names.append(f"dma_{q}_{i}")
        return names

    def set_sems(self, d):
        self.sems = d
        for k in d:
            self.cnt[k] = 0

    def _deps(self, reads, writes):
        deps = []
        for r in reads:
            w = self.last_w.get(r)
            if w is not None:
                deps.append(w)
        for w_ in writes:
            w = self.last_w.get(w_)
            if w is not None:
                deps.append(w)
            deps.extend(self.readers.get(w_, ()))
        return deps

    def _waits(self, eng, deps, skip_same):
        need = {}
        for (k, v) in deps:
            if skip_same and k == eng:
                continue
            if need.get(k, 0) < v:
                need[k] = v
        out = []
        wd = self.waited[eng]
        for k, v in need.items():
            if wd.get(k, 0) >= v:
                continue
            wd[k] = v
            out.append((k, v))
        return out

    def _record(self, tok, reads, writes):
        for r in reads:
            self.readers.setdefault(r, []).append(tok)
        for w in writes:
            self.last_w[w] = tok
            self.readers[w] = []

    def op(self, eng, fn, reads=(), writes=(), skip_same=None):
        if skip_same is None:
            skip_same = (eng == "pe")
        waits = self._waits(eng, self._deps(reads, writes), skip_same)
        self.cnt[eng] += 1
        tok = (eng, self.cnt[eng])
        self.ops[eng].append((waits, fn, eng, 1))
        self._record(tok, reads, writes)
        self.n_instr += 1
        return tok

    def dma(self, q, fn, reads=(), writes=()):
        i = self.dma_rr[q]
        self.dma_rr[q] = (i + 1) % N_DMA_SEMS
        sk = f"dma_{q}_{i}"
        deps = self._deps(reads, writes)
        if self.cnt[sk] > 0:
            deps.append((sk, self.cnt[sk]))
        waits = self._waits(q, deps, False)
        self.cnt[sk] += 16
        tok = (sk, self.cnt[sk])
        self.ops[q].append((waits, fn, sk, 16))
        self._record(tok, reads, writes)
        self.n_instr += 1
        return tok

    def barrier(self):
        for e in ENGS:
            deps = [(k, v) for k, v in self.cnt.items() if v > 0 and k != e]
            waits = self._waits(e, deps, False)
            if waits:
                self.ops[e].append((waits, None, None, 0))
        self.last_w = {}
        self.readers = {}

    def emit(self, block):
        sems = self.sems

        def run(engobj, lst):
            for (waits, fn, sk, amt) in lst:
                for (k, v) in waits:
                    engobj.wait_ge(sems[k], v)
                if fn is not None:
                    fn(engobj).then_inc(sems[sk], amt)

        @block.tensor
        def _(e):
            run(e, self.ops["pe"])

        @block.scalar
        def _(e):
            run(e, self.ops["act"])

        @block.vector
        def _(e):
            run(e, self.ops["dve"])

        @block.gpsimd
        def _(e):
            run(e, self.ops["pool"])

        @block.sync
        def _(e):
            run(e, self.ops["sp"])


def build_nc(L, last_is_output=True, debug=False):
    nc = bass.Bass("TRN2", target_bir_lowering=False)

    def din(name, shape, dt=F32):
        return nc.dram_tensor(name, list(shape), dt, kind="ExternalInput").ap()

    def dscr(name, shape, dt, out=False):
        if out:
            return nc.dram_tensor(name, list(shape), dt, kind="ExternalOutput").ap()
        return nc.dram_tensor(name, list(shape), dt).ap()

    xT_in = din("xT", [D, S])
    w_in = din("w_in", [L, D, IN_W])
    w_out = din("w_out", [L, D, D])
    w_up = din("w_up", [L, D, 2 * DFF])
    w_down = din("w_down", [L, DFF, D])
    gvec_d = din("gvec", [128, L * 3 * KC])
    qkg_d = din("qkg", [128, L * 4])
    sink_d = din("sinks", [128, L * 8])
    convw_d = din("convw", [128, L * 3 * 2 * FC])
    convb_d = din("convb", [128, L * 2 * FC])
    biasA_d = din("biasA", [8, 128, 256])
    biasC_d = din("biasC", [3, 16, 128, 256])
    maskA_d = din("maskA", [128, 256])
    maskC_d = din("maskC", [128, 256])
    m01_d = din("m01", [128, 4 * 512])
    cmat_d = din("cmat", [128, 4 * 128])

    out_T = dscr("outT", [D, S], F32, out=True)
    xmid = dscr("xmid", [D, S], F32, out=debug)
    xres = dscr("xres", [D, S], F32)
    projT = dscr("projT", [29 * 128, S], BF16, out=debug)
    vtok = dscr("vtok", [S, NVCOL], BF16, out=debug)
    mixT = dscr("mixT", [D, S], BF16, out=debug)
    actT = dscr("actT", [DFF, S], BF16, out=debug)

    Sd = Sched(nc)

    with ExitStack() as top:
        uid = [0]

        def sbt(es, name, shape, dt):
            uid[0] += 1
            return es.enter_context(nc.sbuf_tensor(f"{name}_u{uid[0]}", list(shape), dt))

        ps = top.enter_context(nc.psum_tensor("ps", [128, 8, 512], F32))
        sems = {n: top.enter_context(nc.semaphore(n)) for n in Sd.sem_names()}
        Sd.set_sems(sems)

        gvec = sbt(top, "gvec_s", [128, L, 3, KC], F32)
        qkg = sbt(top, "qkg_s", [128, L, 4], F32)
        esink = sbt(top, "esink_s", [128, L, 8], F32)
        convw = sbt(top, "convw_s", [128, L, 3, 2 * FC], F32)
        convb = sbt(top, "convb_s", [128, L, 2 * FC], F32)
        cmat_f = sbt(top, "cmat_f", [128, 4, 128], F32)
        cmat = sbt(top, "cmat_b", [128, 4, 128], BF16)
        eps1 = sbt(top, "eps1", [128, 1], F32)
        eps64 = sbt(top, "eps64", [128, 1], F32)
        ONES, BD1, BD64, LTRI = 0, 1, 2, 3

        Sd.dma("sp", lambda e: e.dma_start(out=gvec[:].rearrange("p a b c -> p (a b c)"), in_=gvec_d), writes=["gvec"])
        Sd.dma("sp", lambda e: e.dma_start(out=qkg[:].rearrange("p a b -> p (a b)"), in_=qkg_d), writes=["qkg"])
        Sd.dma("sp", lambda e: e.dma_start(out=esink[:].rearrange("p a b -> p (a b)"), in_=sink_d), writes=["esink"])
        Sd.dma("sp", lambda e: e.dma_start(out=convw[:].rearrange("p a b c -> p (a b c)"), in_=convw_d), writes=["convw"])
        Sd.dma("sp", lambda e: e.dma_start(out=convb[:].rearrange("p a b -> p (a b)"), in_=convb_d), writes=["convb"])
        Sd.dma("sp", lambda e: e.dma_start(out=cmat_f[:].rearrange("p a b -> p (a b)"), in_=cmat_d), writes=["cmat_f"])
        Sd.op("dve", lambda e: e.tensor_copy(out=cmat[:], in_=cmat_f[:]), reads=["cmat_f"], writes=["cmat"])
        Sd.op("dve", lambda e: e.memset(eps1[:], EPS), writes=["eps1"])
        Sd.op("dve", lambda e: e.memset(eps64[:], 64.0 * EPS), writes=["eps64"])
        Sd.op("act", lambda e: e.activation(out=esink[:], in_=esink[:], func=AF.Exp), reads=["esink"], writes=["esink"])
        Sd.barrier()

        def prologue(es, l, x_src, gi, want_col):
            xb = sbt(es, "xb", [128, KC, S], BF16)
            rstd_bc = sbt(es, "rstd_bc", [128, S], F32)
            rstd_col = sbt(es, "rstd_col", [128, S // 128], F32) if want_col else None
            TT = 256
            with ExitStack() as loc:
                xf = [sbt(loc, f"xf{i}", [128, KC, TT], F32) for i in range(2)]
                sq = [sbt(loc, f"sq{i}", [128, KC, TT], BF16) for i in range(2)]
                lt = [sbt(loc, f"lt{i}", [128, TT], F32) for i in range(2)]
                xsv = x_src.rearrange("(c p) t -> p c t", p=128)
                for it in range(S // TT):
                    b = it % 2
                    t0 = it * TT
                    Sd.dma("sp", lambda e, b=b, t0=t0: e.dma_start(out=xf[b][:], in_=xsv[:, :, t0:t0 + TT]),
                           writes=[("xf", b)])
                    Sd.op("act", lambda e, b=b: e.activation(out=sq[b][:], in_=xf[b][:], func=AF.Square),
                          reads=[("xf", b)], writes=[("sq", b)])
                    for c in range(KC):
                        Sd.op("dve", lambda e, b=b, c=c, t0=t0: e.tensor_scalar(
                            out=xb[:, c, t0:t0 + TT], in0=xf[b][:, c, :], scalar1=gvec[:, l, gi, c:c + 1],
                            scalar2=None, op0=ALU.mult), reads=[("xf", b), "gvec"], writes=[("xb", it)])
                    pb = b
                    for c in range(KC):
                        Sd.op("pe", lambda e, b=b, c=c, pb=pb: e.matmul(
                            ps[:, pb, 0:TT], lhsT=cmat[:, ONES, :], rhs=sq[b][:, c, :], start=(c == 0), stop=(c == KC - 1)),
                            reads=[("sq", b), "cmat"], writes=[("ps", pb)])
                    Sd.op("act", lambda e, b=b, pb=pb: e.activation(
                        out=lt[b][:], in_=ps[:, pb, 0:TT], func=AF.Ln, bias=eps1[:, 0:1], scale=1.0 / D),
                        reads=[("ps", pb), "eps1"], writes=[("lt", b)])
                    Sd.op("act", lambda e, b=b, t0=t0: e.activation(
                        out=rstd_bc[:, t0:t0 + TT], in_=lt[b][:], func=AF.Exp, scale=-0.5),
                        reads=[("lt", b)], writes=[("rstd_bc", it)])
                    if want_col:
                        pc = 2 + b
                        for j in range(TT // 128):
                            for c in range(KC):
                                Sd.op("pe", lambda e, b=b, c=c, j=j, pc=pc: e.matmul(
                                    ps[:, pc, j:j + 1], lhsT=sq[b][:, c, j * 128:(j + 1) * 128], rhs=cmat[:, ONES, 0:1],
                                    start=(c == 0), stop=(c == KC - 1)),
                                    reads=[("sq", b), "cmat"], writes=[("ps", pc)])
                        nj = TT // 128
                        c0 = it * nj
                        Sd.op("act", lambda e, b=b, pc=pc, nj=nj: e.activation(
                            out=lt[b][:, 0:nj], in_=ps[:, pc, 0:nj], func=AF.Ln, bias=eps1[:, 0:1], scale=1.0 / D),
                            reads=[("ps", pc), "eps1", ("lt", b)], writes=[("lt", b)])
                        Sd.op("act", lambda e, b=b, nj=nj, c0=c0: e.activation(
                            out=rstd_col[:, c0:c0 + nj], in_=lt[b][:, 0:nj], func=AF.Exp, scale=-0.5),
                            reads=[("lt", b)], writes=[("rstd_col", it)])
                Sd.barrier()
            return xb, rstd_bc, rstd_col

        XB_ALL = [("xb", it) for it in range(S // 256)]
        RB_ALL = [("rstd_bc", it) for it in range(S // 256)]

        def load_w(wbuf, slot, w_ap, col0, ncols, kc, key):
            wv = w_ap.rearrange("(c p) m -> p c m", p=128)
            Sd.dma("pool", lambda e: e.dma_start(out=wbuf[slot][:, 0:kc, 0:ncols], in_=wv[:, :, col0:col0 + ncols]),
                   writes=[(key, slot)])

        def phase_inproj(l, x_src):
            with ExitStack() as es:
                xb, rstd_bc, rstd_col = prologue(es, l, x_src, 0, True)
                es1 = ExitStack()
                wbuf = [sbt(es1, f"wb{i}", [128, KC, 128], BF16) for i in range(2)]
                stage = [sbt(es1, f"stg{i}", [128, S], BF16) for i in range(2)]
                raw = [sbt(es1, f"raw{i}", [128, 2, 512], F32) for i in range(2)]
                sqq = [sbt(es1, f"sqq{i}", [128, 2, 512], BF16) for i in range(2)]
                lnt = [sbt(es1, f"lnt{i}", [128, 2, 512], F32) for i in range(2)]
                chunks = []
                for i in range(4):
                    chunks.append((i * 128, "q", 0))
                chunks.append((512, "k", 1))
                for i in range(4):
                    chunks.append((768 + i * 128, "bq", None))
                for i in range(4):
                    chunks.append((1280 + i * 128, "p", None))
                for i in range(8):
                    chunks.append((2304 + i * 128, "q", 2))
                for i in range(8):
                    chunks.append((3328 + i * 128, "k", 3))
                assert len(chunks) == 29
                load_w(wbuf, 0, w_in[l], chunks[0][0], 128, KC, "wb")
                grp = 0
                for ci, (col0, kind, gidx) in enumerate(chunks):
                    slot = ci % 2
                    if ci + 1 < len(chunks):
                        load_w(wbuf, 1 - slot, w_in[l], chunks[ci + 1][0], 128, KC, "wb")
                    stg = stage[slot]
                    for g in range(S // 1024):
                        pb = 2 * (grp % 2)
                        rb = grp % 2
                        grp += 1
                        t0 = g * 1024
                        for k in range(KC):
                            for tt in range(2):
                                Sd.op("pe", lambda e, k=k, tt=tt, pb=pb, slot=slot, t0=t0: e.matmul(
                                    ps[:, pb + tt, :], lhsT=wbuf[slot][:, k, :], rhs=xb[:, k, t0 + tt * 512:t0 + (tt + 1) * 512],
                                    start=(k == 0), stop=(k == KC - 1)),
                                    reads=[("wb", slot)] + XB_ALL[4 * g:4 * g + 4], writes=[("ps", pb + tt)])
                        psv = ps[:, pb:pb + 2, :]
                        rbv = rstd_bc[:, t0:t0 + 1024].rearrange("p (a b) -> p a b", a=2)
                        stv = stg[:, t0:t0 + 1024].rearrange("p (a b) -> p a b", a=2)
                        pkeys = [("ps", pb), ("ps", pb + 1)]
                        if kind == "p":
                            Sd.op("dve", lambda e, psv=psv, rbv=rbv, stv=stv: e.tensor_tensor(
                                out=stv, in0=psv, in1=rbv, op=ALU.mult),
                                reads=pkeys + RB_ALL[4 * g:4 * g + 4], writes=[("stg", slot, g)])
                        elif kind == "bq":
                            Sd.op("dve", lambda e, psv=psv, rbv=rbv, stv=stv: e.scalar_tensor_tensor(
                                out=stv, in0=psv, scalar=0.125, in1=rbv, op0=ALU.mult, op1=ALU.mult),
                                reads=pkeys + RB_ALL[4 * g:4 * g + 4], writes=[("stg", slot, g)])
                        else:
                            Sd.op("dve", lambda e, psv=psv, rbv=rbv, rb=rb: e.tensor_tensor(
                                out=raw[rb][:], in0=psv, in1=rbv, op=ALU.mult),
                                reads=pkeys + RB_ALL[4 * g:4 * g + 4], writes=[("raw", rb)])
                            Sd.op("act", lambda e, rb=rb: e.activation(out=sqq[rb][:], in_=raw[rb][:], func=AF.Square),
                                  reads=[("raw", rb)], writes=[("sqq", rb)])
                            mat = BD1 if kind == "q" else BD64
                            for tt in range(2):
                                Sd.op("pe", lambda e, rb=rb, tt=tt, pb=pb, mat=mat: e.matmul(
                                    ps[:, 4 + pb + tt, :], lhsT=cmat[:, mat, :], rhs=sqq[rb][:, tt, :], start=True, stop=True),
                                    reads=[("sqq", rb), "cmat"], writes=[("ps", 4 + pb + tt)])
                            ept = eps64 if kind == "q" else eps1
                            Sd.op("act", lambda e, rb=rb, pb=pb, ept=ept: e.activation(
                                out=lnt[rb][:], in_=ps[:, 4 + pb:6 + pb, :], func=AF.Ln, bias=ept[:, 0:1]),
                                reads=[("ps", 4 + pb), ("ps", 5 + pb), "eps1", "eps64"], writes=[("lnt", rb)])
                            Sd.op("act", lambda e, rb=rb: e.activation(
                                out=lnt[rb][:], in_=lnt[rb][:], func=AF.Exp, scale=-0.5),
                                reads=[("lnt", rb)], writes=[("lnt", rb)])
                            Sd.op("dve", lambda e, rb=rb, stv=stv, gidx=gidx: e.scalar_tensor_tensor(
                                out=stv, in0=raw[rb][:], scalar=qkg[:, l, gidx:gidx + 1], in1=lnt[rb][:],
                                op0=ALU.mult, op1=ALU.mult),
                                reads=[("raw", rb), ("lnt", rb), "qkg"], writes=[("stg", slot, g)])
                    Sd.dma("sp", lambda e, ci=ci, stg=stg: e.dma_start(out=projT[ci * 128:(ci + 1) * 128, :], in_=stg[:]),
                           reads=[("stg", slot, g) for g in range(4)])
                Sd.barrier()
                es1.close()
                wv512 = sbt(es, "wv512", [128, KC, 512], BF16)
                vst = [sbt(es, f"vst{i}", [128, 512], BF16) for i in range(2)]
                vblocks = [(640, 128, 0), (1792, 512, 128), (4352, 512, 640), (4864, 512, 1152)]
                vi = 0
                for (col0, ncols, vc0) in vblocks:
                    wvv = w_in[l].rearrange("(c p) m -> p c m", p=128)
                    Sd.dma("pool", lambda e, col0=col0, ncols=ncols, wvv=wvv: e.dma_start(
                        out=wv512[:, :, 0:ncols], in_=wvv[:, :, col0:col0 + ncols]), writes=["wv512"])
                    for tc_ in range(S // 128):
                        pb = vi % 2
                        sl = vi % 2
                        vi += 1
                        for k in range(KC):
                            Sd.op("pe", lambda e, k=k, pb=pb, tc_=tc_, ncols=ncols: e.matmul(
                                ps[:, pb, 0:ncols], lhsT=xb[:, k, tc_ * 128:(tc_ + 1) * 128], rhs=wv512[:, k, 0:ncols],
                                start=(k == 0), stop=(k == KC - 1)),
                                reads=["wv512", ("xb", tc_ // 2)], writes=[("ps", pb)])
                        Sd.op("dve", lambda e, pb=pb, sl=sl, tc_=tc_, ncols=ncols: e.tensor_scalar(
                            out=vst[sl][:, 0:ncols], in0=ps[:, pb, 0:ncols], scalar1=rstd_col[:, tc_:tc_ + 1],
                            scalar2=None, op0=ALU.mult),
                            reads=[("ps", pb), ("rstd_col", tc_ // 2)], writes=[("vst", sl)])
                        Sd.dma("sp", lambda e, sl=sl, tc_=tc_, ncols=ncols, vc0=vc0: e.dma_start(
                            out=vtok[tc_ * 128:(tc_ + 1) * 128, vc0:vc0 + ncols], in_=vst[sl][:, 0:ncols]),
                            reads=[("vst", sl)])
                Sd.barrier()

        def make_E(E2, idx, bias_ap, mask_t, tmp, tslot):
            Sd.dma("sp", lambda e: e.dma_start(out=tmp[tslot][:], in_=bias_ap), writes=[("etmp", tslot)])
            Sd.op("act", lambda e: e.activation(out=tmp[tslot][:], in_=tmp[tslot][:], func=AF.Exp),
                  reads=[("etmp", tslot)], writes=[("etmp", tslot)])
            tv = tmp[tslot][:].rearrange("p (a b) -> p a b", a=2)
            mv = mask_t[:].rearrange("p (a b) -> p a b", a=2)
            for dup in range(2):
                Sd.op("dve", lambda e, dup=dup: e.tensor_tensor(out=E2[:, idx, :, dup, :], in0=tv, in1=mv, op=ALU.mult),
                      reads=[("etmp", tslot), "mask"], writes=[("E2", idx)])

        def banded(qT, kT, Vget, E2, eidx, dil, acc, first, et, Pt, ctr):
            nb = S // dil // 128
            qv = qT[:].rearrange("p (n r) -> p n r", r=dil)
            kv = kT[:].rearrange("p (n r) -> p n r", r=dil)
            accv = acc[:].rearrange("p a (n r) -> p a n r", r=dil)
            Ev = E2[:, eidx].rearrange("p a b c -> p (a b c)")
            onesl = cmat[:, ONES, 0:64]

            def blk(b):
                return slice(128 * b, 128 * b + 128)

            for r in range(dil):
                for bp in range(nb // 2):
                    b0, b1 = 2 * bp, 2 * bp + 1
                    i = ctr[0]
                    ctr[0] += 1
                    sbk = i % 2
                    obk = 2 + i % 2
                    bf = i % 2
                    mms = []
                    if b0 > 0:
                        mms.append((0, b0 - 1, b0))
                    mms += [(128, b0, b1), (256, b0, b0), (384, b1, b1)]
                    for (c0, kb, qb) in mms:
                        Sd.op("pe", lambda e, c0=c0, kb=kb, qb=qb, sbk=sbk, r=r: e.matmul(
                            ps[:, sbk, c0:c0 + 128], lhsT=kv[:, blk(kb), r], rhs=qv[:, blk(qb), r], start=True, stop=True),
                            reads=["qT", "kT"], writes=[("ps", sbk)])
                    Sd.op("act", lambda e, sbk=sbk, bf=bf: e.activation(out=et[bf][:], in_=ps[:, sbk, :], func=AF.Exp),
                          reads=[("ps", sbk)], writes=[("et", bf)])
                    Sd.op("dve", lambda e, bf=bf: e.tensor_tensor(out=Pt[bf][:], in0=et[bf][:], in1=Ev, op=ALU.mult),
                          reads=[("et", bf), ("E2", eidx)], writes=[("Pt", bf)])
                    for (half, lhs_of) in ((0, None), (256, onesl)):
                        for (qi, pairs) in ((0, [(0, b0 - 1), (256, b0)]), (1, [(128, b0), (384, b1)])):
                            pl = [(pc, kb) for (pc, kb) in pairs if kb >= 0]
                            for n_, (pc, kb) in enumerate(pl):
                                lh = lhs_of if lhs_of is not None else Vget(r, kb)
                                Sd.op("pe", lambda e, lh=lh, pc=pc, obk=obk, half=half, qi=qi, bf=bf, n_=n_, tot=len(pl): e.matmul(
                                    ps[0:64, obk, half + qi * 128:half + qi * 128 + 128], lhsT=lh, rhs=Pt[bf][:, pc:pc + 128],
                                    start=(n_ == 0), stop=(n_ == tot - 1)),
                                    reads=[("Pt", bf), "V", "cmat"], writes=[("ps", obk)])
                    ov = ps[0:64, obk, :].rearrange("p (a b) -> p a b", a=2)
                    av = accv[:, :, 256 * bp:256 * bp + 256, r]
                    lo = 256 * bp * dil
                    akeys = [("acc", q) for q in range(lo // 1024, min(4, (lo + 256 * dil + 1023) // 1024))]
                    if first:
                        Sd.op("act", lambda e, ov=ov, av=av: e.activation(out=av, in_=ov, func=AF.Copy),
                              reads=[("ps", obk)], writes=akeys)
                    else:
                        Sd.op("dve", lambda e, ov=ov, av=av: e.tensor_tensor(out=av, in0=av, in1=ov, op=ALU.add),
                              reads=[("ps", obk)] + akeys, writes=akeys)

        def finish_head(acc, outT, row0, sink_ap, rec):
            AK = [("acc", q) for q in range(4)]
            if sink_ap is not None:
                Sd.op("dve", lambda e: e.tensor_scalar(out=acc[:, 1, :], in0=acc[:, 1, :], scalar1=sink_ap, scalar2=None,
                                                       op0=ALU.add), reads=AK + ["esink"], writes=AK)
            Sd.op("dve", lambda e: e.reciprocal(out=rec[:], in_=acc[:, 1, :]), reads=AK, writes=["rec"])
            Sd.op("dve", lambda e: e.tensor_tensor(out=outT[:], in0=acc[:, 0, :], in1=rec[:], op=ALU.mult),
                  reads=AK + ["rec"], writes=["outT"])
            Sd.dma("sp", lambda e: e.dma_start(out=mixT[row0:row0 + 64, :], in_=outT[:]), reads=["outT"])

        def load_qk(qT, kT, qrow, krow):
            Sd.dma("sp", lambda e: e.dma_start(out=qT[:], in_=projT[qrow:qrow + 64, :]), writes=["qT"])
            if krow is not None:
                Sd.dma("sp", lambda e: e.dma_start(out=kT[:], in_=projT[krow:krow + 64, :]), writes=["kT"])

        def phase_attn(l):
            with ExitStack() as es:
                qT = sbt(es, "qT", [64, S], BF16)
                kT = sbt(es, "kT", [64, S], BF16)
                acc = sbt(es, "acc", [64, 2, S], F32)
                rec = sbt(es, "rec", [64, S], F32)
                outT = sbt(es, "outT", [64, S], BF16)
                et = [sbt(es, f"et{i}", [128, 512], F32) for i in range(2)]
                Pt = [sbt(es, f"Pt{i}", [128, 512], BF16) for i in range(2)]
                etmp = [sbt(es, f"etmp{i}", [128, 256], F32) for i in range(2)]
                maskA = sbt(es, "maskA", [128, 256], F32)
                maskC = sbt(es, "maskC", [128, 256], F32)
                E2 = sbt(es, "E2", [128, 12, 2, 2, 128], F32)
                Vd = [sbt(es, f"Vd{i}", [128, 32, 256], BF16) for i in range(3)]
                ctr = [0]
                Sd.dma("sp", lambda e: e.dma_start(out=maskA[:], in_=maskA_d), writes=["mask"])
                Sd.dma("sp", lambda e: e.dma_start(out=maskC[:], in_=maskC_d), writes=["mask"])
                for h in range(8):
                    make_E(E2, h, biasA_d[h], maskA, etmp, h % 2)
                Sd.dma("sp", lambda e: e.dma_start(
                    out=Vd[0][:, :, 0:128], in_=vtok.rearrange("(b j) c -> j b c", j=128)[:, :, 0:128]), writes=["V"])
                for h in range(8):
                    g = h // 4
                    load_qk(qT, kT, h * 64, (512 + g * 64) if h % 4 == 0 else None)
                    Vget = lambda r, kb, g=g: Vd[0][:, kb, g * 64:(g + 1) * 64]
                    banded(qT, kT, Vget, E2, h, 1, acc, True, et, Pt, ctr)
                    finish_head(acc, outT, h * 64, esink[0:64, l, h:h + 1], rec)
                for hg in range(4):
                    for di, dil in enumerate(DILS):
                        nb = S // dil // 128
                        vsrc = vtok.rearrange("(b j r) c -> j r b c", j=128, r=dil)
                        for r in range(dil):
                            Sd.dma("sp", lambda e, di=di, r=r, nb=nb, hg=hg, vsrc=vsrc: e.dma_start(
                                out=Vd[di][:, r * nb:(r + 1) * nb, :],
                                in_=vsrc[:, r, :, 640 + hg * 256:640 + (hg + 1) * 256]), writes=["V"])
                        for hh in range(4):
                            make_E(E2, di * 4 + hh, biasC_d[di, hg * 4 + hh], maskC, etmp, (di * 4 + hh) % 2)
                    for hh in range(4):
                        h = hg * 4 + hh
                        load_qk(qT, kT, 13 * 128 + h * 64, 21 * 128 + h * 64)
                        for di, dil in enumerate(DILS):
                            nb = S // dil // 128
                            Vget = lambda r, kb, di=di, nb=nb, hh=hh: Vd[di][:, r * nb + kb, hh * 64:(hh + 1) * 64]
                            banded(qT, kT, Vget, E2, di * 4 + hh, dil, acc, di == 0, et, Pt, ctr)
                        finish_head(acc, outT, 1024 + h * 64, None, rec)
                Sd.barrier()
            with ExitStack() as es:
                qT = sbt(es, "qTb", [64, S], BF16)
                kT = sbt(es, "kTb", [64, S], BF16)
                nkT = sbt(es, "nkTb", [64, S], BF16)
                VB = sbt(es, "VB", [128, 32, 512], BF16)
                m01f = sbt(es, "m01f", [128, 4, 512], F32)
                m01 = sbt(es, "m01", [128, 4, 512], BF16)
                ee = [sbt(es, f"ee{i}", [128, 512], F32) for i in range(2)]
                sp_ = [sbt(es, f"sp{i}", [128, 512], BF16) for i in range(2)]
                At = [sbt(es, f"At{i}", [128, 512], BF16) for i in range(2)]
                Trow = sbt(es, "Trow", [1, 512], F32)
                onesr = sbt(es, "onesr", [1, 128], F32)
                outB = [sbt(es, f"outB{i}", [64, 512], BF16) for i in range(2)]
                Sd.dma("sp", lambda e: e.dma_start(out=m01f[:].rearrange("p a b -> p (a b)"), in_=m01_d), writes=["m01f"])
                Sd.op("dve", lambda e: e.tensor_copy(out=m01[:], in_=m01f[:]), reads=["m01f"], writes=["m01"])
                Sd.op("dve", lambda e: e.memset(onesr[:], 1.0), writes=["onesr"])
                Sd.dma("sp", lambda e: e.dma_start(
                    out=VB[:], in_=vtok.rearrange("(b j) c -> j b c", j=128)[:, :, 128:640]), writes=["VB"])
                it = 0
                oi = 0
                for h in range(8):
                    Sd.dma("sp", lambda e, h=h: e.dma_start(out=qT[:], in_=projT[5 * 128 + h * 64:5 * 128 + h * 64 + 64, :]),
                           writes=["qT"])
                    Sd.dma("sp", lambda e, h=h: e.dma_start(out=kT[:], in_=projT[9 * 128 + h * 64:9 * 128 + h * 64 + 64, :]),
                           writes=["kT"])
                    Sd.op("dve", lambda e: e.tensor_scalar(out=nkT[:], in0=kT[:], scalar1=-1.0, scalar2=None, op0=ALU.mult),
                          reads=["kT"], writes=["nkT"])
                    for qt in range(S // 512):
                        qs = slice(qt * 512, (qt + 1) * 512)
                        ob = 4 + oi % 2
                        osl = oi % 2
                        oi += 1
                        kb_top = 4 * qt + 3
                        for kb in range(kb_top, -1, -1):
                            c = kb - 4 * qt
                            b = it % 2
                            it += 1
                            zb = b
                            cb = 2 + b
                            ks = slice(kb * 128, kb * 128 + 128)
                            firstk = (kb == kb_top)
                            Sd.op("pe", lambda e, ks=ks, qs=qs, zb=zb: e.matmul(
                                ps[:, zb, :], lhsT=kT[:, ks], rhs=qT[:, qs], start=True, stop=True),
                                reads=["qT", "kT"], writes=[("ps", zb)])
                            Sd.op("act", lambda e, zb=zb, b=b: e.activation(out=ee[b][:], in_=ps[:, zb, :], func=AF.Exp),
                                  reads=[("ps", zb)], writes=[("ee", b)])
                            Sd.op("act", lambda e, b=b: e.activation(out=sp_[b][:], in_=ee[b][:], func=AF.Ln, bias=1.0),
                                  reads=[("ee", b)], writes=[("sp", b)])
                            if c >= 0:
                                Sd.op("dve", lambda e, b=b, c=c: e.tensor_tensor(
                                    out=sp_[b][:], in0=sp_[b][:], in1=m01[:, c, :], op=ALU.mult),
                                    reads=[("sp", b), "m01"], writes=[("sp", b)])
                            Sd.op("pe", lambda e, b=b, cb=cb: e.matmul(
                                ps[:, cb, :], lhsT=cmat[:, LTRI, :], rhs=sp_[b][:], start=True, stop=False),
                                reads=[("sp", b), "cmat"], writes=[("ps", cb)])
                            if not firstk:
                                Sd.op("pe", lambda e, cb=cb: e.matmul(
                                    ps[:, cb, :], lhsT=onesr[0:1, :], rhs=Trow[0:1, :], start=False, stop=False),
                                    reads=["Trow", "onesr"], writes=[("ps", cb)])
                            Sd.op("pe", lambda e, ks=ks, qs=qs, cb=cb: e.matmul(
                                ps[:, cb, :], lhsT=nkT[:, ks], rhs=qT[:, qs], start=False, stop=True),
                                reads=["qT", "nkT"], writes=[("ps", cb)])
                            if kb > 0:
                                Sd.op("pe", lambda e, b=b: e.matmul(
                                    ps[0:1, 6, :], lhsT=cmat[:, ONES, 0:1], rhs=sp_[b][:], start=True, stop=True),
                                    reads=[("sp", b), "cmat"], writes=[("ps", 6)])
                                if firstk:
                                    Sd.op("dve", lambda e: e.tensor_copy(out=Trow[:], in_=ps[0:1, 6, :]),
                                          reads=[("ps", 6)], writes=["Trow"])
                                else:
                                    Sd.op("dve", lambda e: e.tensor_tensor(out=Trow[:], in0=Trow[:], in1=ps[0:1, 6, :], op=ALU.add),
                                          reads=[("ps", 6), "Trow"], writes=["Trow"])
                            Sd.op("act", lambda e, b=b, cb=cb: e.activation(
                                out=At[b][:], in_=ps[:, cb, :], func=AF.Exp, scale=-1.0),
                                reads=[("ps", cb)], writes=[("At", b)])
                            if c >= 0:
                                Sd.op("dve", lambda e, b=b, c=c: e.tensor_tensor(
                                    out=At[b][:], in0=At[b][:], in1=m01[:, c, :], op=ALU.mult),
                                    reads=[("At", b), "m01"], writes=[("At", b)])
                            Sd.op("pe", lambda e, b=b, ob=ob, kb=kb, h=h, firstk=firstk: e.matmul(
                                ps[0:64, ob, :], lhsT=VB[:, kb, h * 64:(h + 1) * 64], rhs=At[b][:],
                                start=firstk, stop=(kb == 0)),
                                reads=[("At", b), "VB"], writes=[("ps", ob)])
                        Sd.op("dve", lambda e, ob=ob, osl=osl: e.tensor_copy(out=outB[osl][:], in_=ps[0:64, ob, :]),
                              reads=[("ps", ob)], writes=[("outB", osl)])
                        Sd.dma("sp", lambda e, osl=osl, h=h, qs=qs: e.dma_start(
                            out=mixT[512 + h * 64:512 + h * 64 + 64, qs], in_=outB[osl][:]), reads=[("outB", osl)])
                Sd.barrier()

        def phase_outproj(l, x_src, x_dst):
            with ExitStack() as es:
                mixn = sbt(es, "mixn", [128, KC, S], BF16)
                with ExitStack() as loc:
                    MT = 256
                    mt = [sbt(loc, f"mt{i}", [128, KC, MT], BF16) for i in range(2)]
                    sq = [sbt(loc, f"msq{i}", [128, KC, MT], BF16) for i in range(2)]
                    rg = [sbt(loc, f"rg{i}", [128, 3, MT], F32) for i in range(2)]
                    mv = mixT.rearrange("(c p) t -> p c t", p=128)
                    groups = [(0, 4), (4, 8), (8, 16)]
                    for it in range(S // MT):
                        b = it % 2
                        t0 = it * MT
                        Sd.dma("sp", lambda e, b=b, t0=t0: e.dma_start(out=mt[b][:], in_=mv[:, :, t0:t0 + MT]),
                               writes=[("mt", b)])
                        Sd.op("act", lambda e, b=b: e.activation(out=sq[b][:], in_=mt[b][:], func=AF.Square),
                              reads=[("mt", b)], writes=[("msq", b)])
                        for gi, (c0, c1) in enumerate(groups):
                            pb = 3 * b + gi
                            for c in range(c0, c1):
                                Sd.op("pe", lambda e, b=b, c=c, pb=pb, c0=c0, c1=c1: e.matmul(
                                    ps[:, pb, 0:MT], lhsT=cmat[:, ONES, :], rhs=sq[b][:, c, :], start=(c == c0), stop=(c == c1 - 1)),
                                    reads=[("msq", b), "cmat"], writes=[("ps", pb)])
                            Sd.op("act", lambda e, b=b, gi=gi, pb=pb, c0=c0, c1=c1: e.activation(
                                out=rg[b][:, gi, :], in_=ps[:, pb, 0:MT], func=AF.Ln, bias=eps1[:, 0:1],
                                scale=1.0 / ((c1 - c0) * 128)),
                                reads=[("ps", pb), "eps1"], writes=[("rg", b, gi)])
                            Sd.op("act", lambda e, b=b, gi=gi: e.activation(
                                out=rg[b][:, gi, :], in_=rg[b][:, gi, :], func=AF.Exp, scale=-0.5),
                                reads=[("rg", b, gi)], writes=[("rg", b, gi)])
                            for c in range(c0, c1):
                                Sd.op("dve", lambda e, b=b, c=c, gi=gi, t0=t0: e.scalar_tensor_tensor(
                                    out=mixn[:, c, t0:t0 + MT], in0=mt[b][:, c, :], scalar=gvec[:, l, 1, c:c + 1],
                                    in1=rg[b][:, gi, :], op0=ALU.mult, op1=ALU.mult),
                                    reads=[("mt", b), ("rg", b, gi), "gvec"], writes=[("mixn", it)])
                    Sd.barrier()
                wbuf = [sbt(es, f"wbo{i}", [128, KC, 128], BF16) for i in range(2)]
                xo = [sbt(es, f"xo{i}", [128, 2, 512], F32) for i in range(2)]
                load_w(wbuf, 0, w_out[l], 0, 128, KC, "wb")
                grp = 0
                for mc in range(KC):
                    slot = mc % 2
                    if mc + 1 < KC:
                        load_w(wbuf, 1 - slot, w_out[l], (mc + 1) * 128, 128, KC, "wb")
                    for g in range(S // 1024):
                        pb = 2 * (grp % 2)
                        xs_ = grp % 2
                        grp += 1
                        t0 = g * 1024
                        Sd.dma("sp", lambda e, xs_=xs_, mc=mc, t0=t0: e.dma_start(
                            out=xo[xs_][:].rearrange("p a b -> p (a b)"), in_=x_src[mc * 128:(mc + 1) * 128, t0:t0 + 1024]),
                            writes=[("xo", xs_)])
                        for k in range(KC):
                            for tt in range(2):
                                Sd.op("pe", lambda e, k=k, tt=tt, pb=pb, slot=slot, t0=t0: e.matmul(
                                    ps[:, pb + tt, :], lhsT=wbuf[slot][:, k, :], rhs=mixn[:, k, t0 + tt * 512:t0 + (tt + 1) * 512],
                                    start=(k == 0), stop=(k == KC - 1)),
                                    reads=[("wb", slot)], writes=[("ps", pb + tt)])
                        Sd.op("dve", lambda e, xs_=xs_, pb=pb: e.tensor_tensor(
                            out=xo[xs_][:], in0=ps[:, pb:pb + 2, :], in1=xo[xs_][:], op=ALU.add),
                            reads=[("ps", pb), ("ps", pb + 1), ("xo", xs_)], writes=[("xo", xs_)])
                        Sd.dma("sp", lambda e, xs_=xs_, mc=mc, t0=t0: e.dma_start(
                            out=x_dst[mc * 128:(mc + 1) * 128, t0:t0 + 1024], in_=xo[xs_][:].rearrange("p a b -> p (a b)")),
                            reads=[("xo", xs_)])
                Sd.barrier()

        def phase_ffn_up(l, x_src):
            with ExitStack() as es:
                xb, rstd_bc, _ = prologue(es, l, x_src, 2, False)
                wbuf = [sbt(es, f"wbu{i}", [128, KC, 128], BF16) for i in range(4)]
                U = [sbt(es, f"U{i}", [128, 1026], F32) for i in range(2)]
                cv = [sbt(es, f"cv{i}", [128, 1024], F32) for i in range(2)]
                sg = sbt(es, "sg", [128, 1024], F32)
                stage = [sbt(es, f"stgu{i}", [128, 1024], BF16) for i in range(2)]

                def load_j(j):
                    for part in range(2):
                        load_w(wbuf, (j % 2) * 2 + part, w_up[l], part * DFF + j * 128, 128, KC, "wb")

                load_j(0)
                grp = 0
                si = 0
                for j in range(FC):
                    if j + 1 < FC:
                        load_j(j + 1)
                    for part in range(2):
                        Sd.op("dve", lambda e, part=part: e.memset(U[part][:, 0:2], 0.0), writes=[("U", part)])
                    for g in range(S // 1024):
                        t0 = g * 1024
                        ss = si % 2
                        si += 1
                        for part in range(2):
                            slot = (j % 2) * 2 + part
                            pb = 2 * (grp % 2)
                            grp += 1
                            fcol = part * FC + j
                            for k in range(KC):
                                for tt in range(2):
                                    Sd.op("pe", lambda e, k=k, tt=tt, pb=pb, slot=slot, t0=t0: e.matmul(
                                        ps[:, pb + tt, :], lhsT=wbuf[slot][:, k, :], rhs=xb[:, k, t0 + tt * 512:t0 + (tt + 1) * 512],
                                        start=(k == 0), stop=(k == KC - 1)),
                                        reads=[("wb", slot)], writes=[("ps", pb + tt)])
                            Sd.op("dve", lambda e, part=part, pb=pb, t0=t0: e.tensor_tensor(
                                out=U[part][:, 2:1026].rearrange("p (a b) -> p a b", a=2), in0=ps[:, pb:pb + 2, :],
                                in1=rstd_bc[:, t0:t0 + 1024].rearrange("p (a b) -> p a b", a=2), op=ALU.mult),
                                reads=[("ps", pb), ("ps", pb + 1)], writes=[("U", part)])
                            Sd.op("act", lambda e, part=part, fcol=fcol: e.activation(
                                out=cv[part][:], in_=U[part][:, 2:1026], func=AF.Identity,
                                bias=convb[:, l, fcol:fcol + 1], scale=convw[:, l, 2, fcol:fcol + 1]),
                                reads=[("U", part)], writes=[("cv", part)])
                            Sd.op("dve", lambda e, part=part, fcol=fcol: e.scalar_tensor_tensor(
                                out=cv[part][:], in0=U[part][:, 1:1025], scalar=convw[:, l, 1, fcol:fcol + 1], in1=cv[part][:],
                                op0=ALU.mult, op1=ALU.add),
                                reads=[("U", part), ("cv", part)], writes=[("cv", part)])
                            Sd.op("dve", lambda e, part=part, fcol=fcol: e.scalar_tensor_tensor(
                                out=cv[part][:], in0=U[part][:, 0:1024], scalar=convw[:, l, 0, fcol:fcol + 1], in1=cv[part][:],
                                op0=ALU.mult, op1=ALU.add),
                                reads=[("U", part), ("cv", part)], writes=[("cv", part)])
                            Sd.op("dve", lambda e, part=part: e.tensor_copy(out=U[part][:, 0:2], in_=U[part][:, 1024:1026]),
                                  reads=[("U", part)], writes=[("U", part)])
                            if part == 0:
                                Sd.op("act", lambda e: e.activation(out=sg[:], in_=cv[0][:], func=AF.Silu),
                                      reads=[("cv", 0)], writes=["sg"])
                            else:
                                Sd.op("dve", lambda e, ss=ss: e.tensor_tensor(
                                    out=stage[ss][:], in0=sg[:], in1=cv[1][:], op=ALU.mult),
                                    reads=["sg", ("cv", 1)], writes=[("stgu", ss)])
                        Sd.dma("sp", lambda e, j=j, ss=ss, t0=t0: e.dma_start(
                            out=actT[j * 128:(j + 1) * 128, t0:t0 + 1024], in_=stage[ss][:]), reads=[("stgu", ss)])
                Sd.barrier()

        def phase_ffn_down(l, x_src, x_dst):
            with ExitStack() as es:
                at = sbt(es, "at", [128, FC, 1024], BF16)
                wbuf = [sbt(es, f"wbd{i}", [128, FC, 128], BF16) for i in range(2)]
                xo = [sbt(es, f"xod{i}", [128, 2, 512], F32) for i in range(2)]
                av = actT.rearrange("(c p) t -> p c t", p=128)
                wdv = w_down[l].rearrange("(c p) m -> p c m", p=128)
                n = 0
                def load_wd(slot, mc):
                    for qq in range(4):
                        Sd.dma("pool", lambda e, qq=qq: e.dma_start(
                            out=wbuf[slot][:, qq * 11:(qq + 1) * 11, :], in_=wdv[:, qq * 11:(qq + 1) * 11, mc * 128:(mc + 1) * 128]),
                            writes=[("wb", slot, qq)])
                load_wd(0, 0)
                for tg in range(S // 1024):
                    t0 = tg * 1024
                    for hc in range(4):
                        Sd.dma("sp", lambda e, hc=hc, t0=t0: e.dma_start(
                            out=at[:, hc * 11:(hc + 1) * 11, :], in_=av[:, hc * 11:(hc + 1) * 11, t0:t0 + 1024]),
                            writes=[("at", hc)])
                    for mc in range(KC):
                        slot = n % 2
                        pb = 2 * (n % 2)
                        n += 1
                        if not (tg == S // 1024 - 1 and mc == KC - 1):
                            nmc = (mc + 1) % KC
                            load_wd(1 - slot, nmc)
                        Sd.dma("sp", lambda e, slot=slot, mc=mc, t0=t0: e.dma_start(
                            out=xo[slot][:].rearrange("p a b -> p (a b)"), in_=x_src[mc * 128:(mc + 1) * 128, t0:t0 + 1024]),
                            writes=[("xo", slot)])
                        for k in range(FC):
                            for tt in range(2):
                                Sd.op("pe", lambda e, k=k, tt=tt, pb=pb, slot=slot: e.matmul(
                                    ps[:, pb + tt, :], lhsT=wbuf[slot][:, k, :], rhs=at[:, k, tt * 512:(tt + 1) * 512],
                                    start=(k == 0), stop=(k == FC - 1)),
                                    reads=[("wb", slot, k // 11), ("at", k // 11)], writes=[("ps", pb + tt)])
                        Sd.op("dve", lambda e, slot=slot, pb=pb: e.tensor_tensor(
                            out=xo[slot][:], in0=ps[:, pb:pb + 2, :], in1=xo[slot][:], op=ALU.add),
                            reads=[("ps", pb), ("ps", pb + 1), ("xo", slot)], writes=[("xo", slot)])
                        Sd.dma("sp", lambda e, slot=slot, mc=mc, t0=t0: e.dma_start(
                            out=x_dst[mc * 128:(mc + 1) * 128, t0:t0 + 1024], in_=xo[slot][:].rearrange("p a b -> p (a b)")),
                            reads=[("xo", slot)])
                Sd.barrier()

        for l in range(L):
            x0 = xT_in if l == 0 else xres
            x2 = out_T if (l == L - 1) else xres
            phase_inproj(l, x0)
            phase_attn(l)
            phase_outproj(l, x0, xmid)
            phase_ffn_up(l, xmid)
            phase_ffn_down(l, xmid, x2)
        Sd.barrier()
        with nc.Block() as block:
            Sd.emit(block)
    return nc, Sd


def _t5_bucket(dist):
    max_exact = N_BUCKETS // 2
    d = np.maximum(dist, 0)
    df = np.maximum(d, 1).astype(np.float32)
    large = max_exact + (np.log(df / np.float32(max_exact)) / np.float32(math.log(T5_MAX / max_exact))
                         * np.float32(N_BUCKETS - max_exact)).astype(np.int32)
    large = np.minimum(large, N_BUCKETS - 1)
    return np.where(d < max_exact, d, large)


def _host_consts(rel_bias_table):
    j = np.arange(128)[:, None]
    i = np.arange(128)[None, :]
    rel_prev = i + 128 - j
    rel_cur = i - j
    rel = np.stack([rel_prev, rel_cur], axis=1)
    tab = np.asarray(rel_bias_table, np.float32)

    def bias_for(dil, heads, maxd):
        valid = (rel >= 0) & (rel <= maxd)
        bk = _t5_bucket(np.clip(rel, 0, 255) * dil)
        out = np.zeros((len(heads), 128, 2, 128), np.float32)
        for n, h in enumerate(heads):
            out[n] = np.where(valid, tab[bk, h], np.float32(0.0))
        return out.reshape(len(heads), 128, 256), valid.astype(np.float32).reshape(128, 256)

    biasA, maskA = bias_for(1, list(range(8)), 127)
    bc = []
    for dil in DILS:
        b, maskC = bias_for(dil, list(range(8, 24)), 128)
        bc.append(b)
    biasC = np.stack(bc, 0)
    s_ = np.arange(128)[:, None]
    t_ = np.arange(512)[None, :]
    m01 = np.stack([((c * 128 + s_) < t_).astype(np.float32) for c in range(4)], axis=1).reshape(128, 4 * 512)
    ones = np.ones((128, 128), np.float32)
    hd = np.arange(128) // 64
    bd = (hd[:, None] == hd[None, :]).astype(np.float32)
    ltri = (np.arange(128)[:, None] >= np.arange(128)[None, :]).astype(np.float32)
    cmat = np.stack([ones, bd, bd / np.float32(64.0), ltri], axis=1).reshape(128, 4 * 128)
    return dict(biasA=biasA, biasC=biasC, maskA=maskA, maskC=maskC, m01=np.ascontiguousarray(m01),
                cmat=np.ascontiguousarray(cmat))


def _layer_small(attn_norm, mix_out_gain, ffn_norm, a_q_gain, a_k_gain, c_q_gain, c_k_gain, a_sinks, conv_w, conv_b):
    L = attn_norm.shape[0]
    g3 = np.stack([attn_norm, mix_out_gain, ffn_norm], axis=1)
    gvec = g3.reshape(L, 3, KC, 128).transpose(3, 0, 1, 2).reshape(128, L * 3 * KC)
    qk = np.stack([a_q_gain, a_k_gain, c_q_gain, c_k_gain], axis=1)
    qkg = np.concatenate([qk, qk], axis=2).transpose(2, 0, 1).reshape(128, L * 4)
    sinks = np.broadcast_to(a_sinks.reshape(1, L * 8), (128, L * 8))
    cw = conv_w.reshape(L, 3, 2 * FC, 128).transpose(3, 0, 1, 2).reshape(128, L * 3 * 2 * FC)
    cb = conv_b.reshape(L, 2 * FC, 128).transpose(2, 0, 1).reshape(128, L * 2 * FC)
    f = lambda a: np.ascontiguousarray(a, dtype=np.float32)
    return dict(gvec=f(gvec), qkg=f(qkg), sinks=f(sinks), convw=f(cw), convb=f(cb))


_NC_CACHE = {}


def _get_nc(L):
    if L not in _NC_CACHE:
        _NC_CACHE[L] = build_nc(L)[0]
    return _NC_CACHE[L]


LAYERS_PER_LAUNCH = 4


def kernel(x, attn_norm, w_in, a_q_gain, a_k_gain, a_sinks, c_q_gain, c_k_gain, rel_bias_table,
           mix_out_gain, w_out, ffn_norm, w_up, conv_w, conv_b, w_down):
    x = np.asarray(x, np.float32)
    B = x.shape[0]
    depth = w_in.shape[0]
    consts = _host_consts(rel_bias_table)
    xT = [np.ascontiguousarray(x[b].T) for b in range(B)]
    LPL = LAYERS_PER_LAUNCH
    nc = _get_nc(LPL)
    for l0 in range(0, depth, LPL):
        sl = slice(l0, l0 + LPL)
        small = _layer_small(*[np.asarray(a, np.float32)[sl] for a in
                               (attn_norm, mix_out_gain, ffn_norm, a_q_gain, a_k_gain, c_q_gain, c_k_gain,
                                a_sinks, conv_w, conv_b)])
        shared = dict(w_in=np.ascontiguousarray(np.asarray(w_in, np.float32)[sl]),
                      w_out=np.ascontiguousarray(np.asarray(w_out, np.float32)[sl]),
                      w_up=np.ascontiguousarray(np.asarray(w_up, np.float32)[sl]),
                      w_down=np.ascontiguousarray(np.asarray(w_down, np.float32)[sl]))
        shared.update(small)
        shared.update(consts)
        in_maps = [dict(shared, xT=xT[b]) for b in range(B)]
        res = run_bass_kernel_spmd(nc, in_maps, core_ids=list(range(B)))
        xT = [res.results[b]["outT"] for b in range(B)]
    out = np.stack([np.ascontiguousarray(xT[b].T) for b in range(B)], axis=0)
    return out.astype(np.float32)
```

```python
import math
from contextlib import ExitStack

import numpy as np
import concourse.bass as bass
import concourse.mybir as mybir
from concourse.bass_utils import run_bass_kernel_spmd

F32 = mybir.dt.float32
BF16 = mybir.dt.bfloat16
AF = mybir.ActivationFunctionType
ALU = mybir.AluOpType

D = 2048
S = 4096
KC = D // 128
HD = 64
IN_W = 5376
DFF = 5632
FC = DFF // 128
EPS = 1e-6
N_BUCKETS = 32
T5_MAX = 2048
DILS = (1, 4, 16)
NVCOL = 1664

ENGS = ["pe", "act", "dve", "pool", "sp"]
N_DMA_SEMS = 6


class Sched:
    def __init__(self, nc):
        self.nc = nc
        self.ops = {e: [] for e in ENGS}
        self.cnt = {}
        self.waited = {e: {} for e in ENGS}
        self.last_w = {}
        self.readers = {}
        self.sems = {}
        self.dma_rr = {"sp": 0, "pool": 0, "act": 0}
        self.n_instr = 0

    def sem_names(self):
        names = list(ENGS)
        for q in ("sp", "pool", "act"):
            for i in range(N_DMA_SEMS):
                names.append(f"dma_{q}_{i}")
        return names

    def set_sems(self, d):
        self.sems = d
        for k in d:
            self.cnt[k] = 0

    def _deps(self, reads, writes):
        deps = []
        for r in reads:
            w = self.last_w.get(r)
            if w is not None:
                deps.append(w)
        for w_ in writes:
            w = self.last_w.get(w_)
            if w is not None:
                deps.append(w)
            deps.extend(self.readers.get(w_, ()))
        return deps

    def _waits(self, eng, deps, skip_same):
        need = {}
        for (k, v) in deps:
            if skip_same and k == eng:
                continue
            if need.get(k, 0) < v:
                need[k] = v
        out = []
        wd = self.waited[eng]
        for k, v in need.items():
            if wd.get(k, 0) >= v:
                continue
            wd[k] = v
            out.append((k, v))
        return out

    def _record(self, tok, reads, writes):
        for r in reads:
            self.readers.setdefault(r, []).append(tok)
        for w in writes:
            self.last_w[w] = tok
            self.readers[w] = []

    def op(self, eng, fn, reads=(), writes=(), skip_same=None):
        if skip_same is None:
            skip_same = (eng == "pe")
        waits = self._waits(eng, self._deps(reads, writes), skip_same)
        self.cnt[eng] += 1
        tok = (eng, self.cnt[eng])
        self.ops[eng].append((waits, fn, eng, 1))
        self._record(tok, reads, writes)
        self.n_instr += 1
        return tok

    def dma(self, q, fn, reads=(), writes=()):
        i = self.dma_rr[q]
        self.dma_rr[q] = (i + 1) % N_DMA_SEMS
        sk = f"dma_{q}_{i}"
        deps = self._deps(reads, writes)
        if self.cnt[sk] > 0:
            deps.append((sk, self.cnt[sk]))
        waits = self._waits(q, deps, False)
        self.cnt[sk] += 16
        tok = (sk, self.cnt[sk])
        self.ops[q].append((waits, fn, sk, 16))
        self._record(tok, reads, writes)
        self.n_instr += 1
        return tok

    def barrier(self):
        for e in ENGS:
            deps = [(k, v) for k, v in self.cnt.items() if v > 0 and k != e]
            waits = self._waits(e, deps, False)
            if waits:
                self.ops[e].append((waits, None, None, 0))
        self.last_w = {}
        self.readers = {}

    def emit(self, block):
        sems = self.sems

        def run(engobj, lst):
            for (waits, fn, sk, amt) in lst:
                for (k, v) in waits:
                    engobj.wait_ge(sems[k], v)
                if fn is not None:
                    fn(engobj).then_inc(sems[sk], amt)

        @block.tensor
        def _(e):
            run(e, self.ops["pe"])

        @block.scalar
        def _(e):
            run(e, self.ops["act"])

        @block.vector
        def _(e):
            run(e, self.ops["dve"])

        @block.gpsimd
        def _(e):
            run(e, self.ops["pool"])

        @block.sync
        def _(e):
            run(e, self.ops["sp"])


def build_nc(L, last_is_output=True, debug=False):
    nc = bass.Bass("TRN2", target_bir_lowering=False)

    def din(name, shape, dt=F32):
        return nc.dram_tensor(name, list(shape), dt, kind="ExternalInput").ap()

    def dscr(name, shape, dt, out=False):
        if out:
            return nc.dram_tensor(name, list(shape), dt, kind="ExternalOutput").ap()
        return nc.dram_tensor(name, list(shape), dt).ap()

    xT_in = din("xT", [D, S])
    w_in = din("w_in", [L, D, IN_W])
    w_out = din("w_out", [L, D, D])
    w_up = din("w_up", [L, D, 2 * DFF])
    w_down = din("w_down", [L, DFF, D])
    gvec_d = din("gvec", [128, L * 3 * KC])
    qkg_d = din("qkg", [128, L * 4])
    sink_d = din("sinks", [128, L * 8])
    convw_d = din("convw", [128, L * 3 * 2 * FC])
    convb_d = din("convb", [128, L * 2 * FC])
    biasA_d = din("biasA", [8, 128, 256])
    biasC_d = din("biasC", [3, 16, 128, 256])
    maskA_d = din("maskA", [128, 256])
    maskC_d = din("maskC", [128, 256])
    m01_d = din("m01", [128, 4 * 512])
    cmat_d = din("cmat", [128, 4 * 128])

    out_T = dscr("outT", [D, S], F32, out=True)
    xmid = dscr("xmid", [D, S], F32, out=debug)
    xres = dscr("xres", [D, S], F32)
    projT = dscr("projT", [29 * 128, S], BF16, out=debug)
    vtok = dscr("vtok", [S, NVCOL], BF16, out=debug)
    mixT = dscr("mixT", [D, S], BF16, out=debug)
    actT = dscr("actT", [DFF, S], BF16, out=debug)

    Sd = Sched(nc)

    with ExitStack() as top:
        uid = [0]

        def sbt(es, name, shape, dt):
            uid[0] += 1
            return es.enter_context(nc.sbuf_tensor(f"{name}_u{uid[0]}", list(shape), dt))

        ps = top.enter_context(nc.psum_tensor("ps", [128, 8, 512], F32))
        sems = {n: top.enter_context(nc.semaphore(n)) for n in Sd.sem_names()}
        Sd.set_sems(sems)

        gvec = sbt(top, "gvec_s", [128, L, 3, KC], F32)
        qkg = sbt(top, "qkg_s", [128, L, 4], F32)
        esink = sbt(top, "esink_s", [128, L, 8], F32)
        convw = sbt(top, "convw_s", [128, L, 3, 2 * FC], F32)
        convb = sbt(top, "convb_s", [128, L, 2 * FC], F32)
        cmat_f = sbt(top, "cmat_f", [128, 4, 128], F32)
        cmat = sbt(top, "cmat_b", [128, 4, 128], BF16)
        eps1 = sbt(top, "eps1", [128, 1], F32)
        eps64 = sbt(top, "eps64", [128, 1], F32)
        ONES, BD1, BD64, LTRI = 0, 1, 2, 3

        Sd.dma("sp", lambda e: e.dma_start(out=gvec[:].rearrange("p a b c -> p (a b c)"), in_=gvec_d), writes=["gvec"])
        Sd.dma("sp", lambda e: e.dma_start(out=qkg[:].rearrange("p a b -> p (a b)"), in_=qkg_d), writes=["qkg"])
        Sd.dma("sp", lambda e: e.dma_start(out=esink[:].rearrange("p a b -> p (a b)"), in_=sink_d), writes=["esink"])
        Sd.dma("sp", lambda e: e.dma_start(out=convw[:].rearrange("p a b c -> p (a b c)"), in_=convw_d), writes=["convw"])
        Sd.dma("sp", lambda e: e.dma_start(out=convb[:].rearrange("p a b -> p (a b)"), in_=convb_d), writes=["convb"])
        Sd.dma("sp", lambda e: e.dma_start(out=cmat_f[:].rearrange("p a b -> p (a b)"), in_=cmat_d), writes=["cmat_f"])
        Sd.op("dve", lambda e: e.tensor_copy(out=cmat[:], in_=cmat_f[:]), reads=["cmat_f"], writes=["cmat"])
        Sd.op("dve", lambda e: e.memset(eps1[:], EPS), writes=["eps1"])
        Sd.op("dve", lambda e: e.memset(eps64[:], 64.0 * EPS), writes=["eps64"])
        Sd.op("act", lambda e: e.activation(out=esink[:], in_=esink[:], func=AF.Exp), reads=["esink"], writes=["esink"])
        Sd.barrier()

        def prologue(es, l, x_src, gi, want_col):
            xb = sbt(es, "xb", [128, KC, S], BF16)
            rstd_bc = sbt(es, "rstd_bc", [128, S], F32)
            rstd_col = sbt(es, "rstd_col", [128, S // 128], F32) if want_col else None
            TT = 256
            with ExitStack() as loc:
                xf = [sbt(loc, f"xf{i}", [128, KC, TT], F32) for i in range(2)]
                sq = [sbt(loc, f"sq{i}", [128, KC, TT], BF16) for i in range(2)]
                lt = [sbt(loc, f"lt{i}", [128, TT], F32) for i in range(2)]
                xsv = x_src.rearrange("(c p) t -> p c t", p=128)
                for it in range(S // TT):
                    b = it % 2
                    t0 = it * TT
                    Sd.dma("sp", lambda e, b=b, t0=t0: e.dma_start(out=xf[b][:], in_=xsv[:, :, t0:t0 + TT]),
                           writes=[("xf", b)])
                    Sd.op("act", lambda e, b=b: e.activation(out=sq[b][:], in_=xf[b][:], func=AF.Square),
                          reads=[("xf", b)], writes=[("sq", b)])
                    for c in range(KC):
                        Sd.op("dve", lambda e, b=b, c=c, t0=t0: e.tensor_scalar(
                            out=xb[:, c, t0:t0 + TT], in0=xf[b][:, c, :], scalar1=gvec[:, l, gi, c:c + 1],
                            scalar2=None, op0=ALU.mult), reads=[("xf", b), "gvec"], writes=[("xb", it)])
                    pb = b
                    for c in range(KC):
                        Sd.op("pe", lambda e, b=b, c=c, pb=pb: e.matmul(
                            ps[:, pb, 0:TT], lhsT=cmat[:, ONES, :], rhs=sq[b][:, c, :], start=(c == 0), stop=(c == KC - 1)),
                            reads=[("sq", b), "cmat"], writes=[("ps", pb)])
                    Sd.op("act", lambda e, b=b, pb=pb: e.activation(
                        out=lt[b][:], in_=ps[:, pb, 0:TT], func=AF.Ln, bias=eps1[:, 0:1], scale=1.0 / D),
                        reads=[("ps", pb), "eps1"], writes=[("lt", b)])
                    Sd.op("act", lambda e, b=b, t0=t0: e.activation(
                        out=rstd_bc[:, t0:t0 + TT], in_=lt[b][:], func=AF.Exp, scale=-0.5),
                        reads=[("lt", b)], writes=[("rstd_bc", it)])
                    if want_col:
                        pc = 2 + b
                        for j in range(TT // 128):
                            for c in range(KC):
                                Sd.op("pe", lambda e, b=b, c=c, j=j, pc=pc: e.matmul(
                                    ps[:, pc, j:j + 1], lhsT=sq[b][:, c, j * 128:(j + 1) * 128], rhs=cmat[:, ONES, 0:1],
                                    start=(c == 0), stop=(c == KC - 1)),
                                    reads=[("sq", b), "cmat"], writes=[("ps", pc)])
                        nj = TT // 128
                        c0 = it * nj
                        Sd.op("act", lambda e, b=b, pc=pc, nj=nj: e.activation(
                            out=lt[b][:, 0:nj], in_=ps[:, pc, 0:nj], func=AF.Ln, bias=eps1[:, 0:1], scale=1.0 / D),
                            reads=[("ps", pc), "eps1", ("lt", b)], writes=[("lt", b)])
                        Sd.op("act", lambda e, b=b, nj=nj, c0=c0: e.activation(
                            out=rstd_col[:, c0:c0 + nj], in_=lt[b][:, 0:nj], func=AF.Exp, scale=-0.5),
                            reads=[("lt", b)], writes=[("rstd_col", it)])
                Sd.barrier()
            return xb, rstd_bc, rstd_col

        XB_ALL = [("xb", it) for it in range(S // 256)]
        RB_ALL = [("rstd_bc", it) for it in range(S // 256)]

        def load_w(wbuf, slot, w_ap, col0, ncols, kc, key):
            wv = w_ap.rearrange("(c p) m -> p c m", p=128)
            Sd.dma("pool", lambda e: e.dma_start(out=wbuf[slot][:, 0:kc, 0:ncols], in_=wv[:, :, col0:col0 + ncols]),
                   writes=[(key, slot)])

        def phase_inproj(l, x_src):
            with ExitStack() as es:
                xb, rstd_bc, rstd_col = prologue(es, l, x_src, 0, True)
                es1 = ExitStack()
                wbuf = [sbt(es1, f"wb{i}", [128, KC, 128], BF16) for i in range(2)]
                stage = [sbt(es1, f"stg{i}", [128, S], BF16) for i in range(2)]
                raw = [sbt(es1, f"raw{i}", [128, 2, 512], F32) for i in range(2)]
                sqq = [sbt(es1, f"sqq{i}", [128, 2, 512], BF16) for i in range(2)]
                lnt = [sbt(es1, f"lnt{i}", [128, 2, 512], F32) for i in range(2)]
                chunks = []
                for i in range(4):
                    chunks.append((i * 128, "q", 0))
                chunks.append((512, "k", 1))
                for i in range(4):
                    chunks.append((768 + i * 128, "bq", None))
                for i in range(4):
                    chunks.append((1280 + i * 128, "p", None))
                for i in range(8):
                    chunks.append((2304 + i * 128, "q", 2))
                for i in range(8):
                    chunks.append((3328 + i * 128, "k", 3))
                assert len(chunks) == 29
                load_w(wbuf, 0, w_in[l], chunks[0][0], 128, KC, "wb")
                grp = 0
                for ci, (col0, kind, gidx) in enumerate(chunks):
                    slot = ci % 2
                    if ci + 1 < len(chunks):
                        load_w(wbuf, 1 - slot, w_in[l], chunks[ci + 1][0], 128, KC, "wb")
                    stg = stage[slot]
                    for g in range(S // 1024):
                        pb = 2 * (grp % 2)
                        rb = grp % 2
                        grp += 1
                        t0 = g * 1024
                        for k in range(KC):
                            for tt in range(2):
                                Sd.op("pe", lambda e, k=k, tt=tt, pb=pb, slot=slot, t0=t0: e.matmul(
                                    ps[:, pb + tt, :], lhsT=wbuf[slot][:, k, :], rhs=xb[:, k, t0 + tt * 512:t0 + (tt + 1) * 512],
                                    start=(k == 0), stop=(k == KC - 1)),
                                    reads=[("wb", slot)] + XB_ALL[4 * g:4 * g + 4], writes=[("ps", pb + tt)])
                        psv = ps[:, pb:pb + 2, :]
                        rbv = rstd_bc[:, t0:t0 + 1024].rearrange("p (a b) -> p a b", a=2)
                        stv = stg[:, t0:t0 + 1024].rearrange("p (a b) -> p a b", a=2)
                        pkeys = [("ps", pb), ("ps", pb + 1)]
                        if kind == "p":
                            Sd.op("dve", lambda e, psv=psv, rbv=rbv, stv=stv: e.tensor_tensor(
                                out=stv, in0=psv, in1=rbv, op=ALU.mult),
                                reads=pkeys + RB_ALL[4 * g:4 * g + 4], writes=[("stg", slot, g)])
                        elif kind == "bq":
                            Sd.op("dve", lambda e, psv=psv, rbv=rbv, stv=stv: e.scalar_tensor_tensor(
                                out=stv, in0=psv, scalar=0.125, in1=rbv, op0=ALU.mult, op1=ALU.mult),
                                reads=pkeys + RB_ALL[4 * g:4 * g + 4], writes=[("stg", slot, g)])
                        else:
                            Sd.op("dve", lambda e, psv=psv, rbv=rbv, rb=rb: e.tensor_tensor(
                                out=raw[rb][:], in0=psv, in1=rbv, op=ALU.mult),
                                reads=pkeys + RB_ALL[4 * g:4 * g + 4], writes=[("raw", rb)])
                            Sd.op("act", lambda e, rb=rb: e.activation(out=sqq[rb][:], in_=raw[rb][:], func=AF.Square),
                                  reads=[("raw", rb)], writes=[("sqq", rb)])
                            mat = BD1 if kind == "q" else BD64
                            for tt in range(2):
                                Sd.op("pe", lambda e, rb=rb, tt=tt, pb=pb, mat=mat: e.matmul(
                                    ps[:, 4 + pb + tt, :], lhsT=cmat[:, mat, :], rhs=sqq[rb][:, tt, :], start=True, stop=True),
                                    reads=[("sqq", rb), "cmat"], writes=[("ps", 4 + pb + tt)])
                            ept = eps64 if kind == "q" else eps1
                            Sd.op("act", lambda e, rb=rb, pb=pb, ept=ept: e.activation(
                                out=lnt[rb][:], in_=ps[:, 4 + pb:6 + pb, :], func=AF.Ln, bias=ept[:, 0:1]),
                                reads=[("ps", 4 + pb), ("ps", 5 + pb), "eps1", "eps64"], writes=[("lnt", rb)])
                            Sd.op("act", lambda e, rb=rb: e.activation(
                                out=lnt[rb][:], in_=lnt[rb][:], func=AF.Exp, scale=-0.5),
                                reads=[("lnt", rb)], writes=[("lnt", rb)])
                            Sd.op("dve", lambda e, rb=rb, stv=stv, gidx=gidx: e.scalar_tensor_tensor(
                                out=stv, in0=raw[rb][:], scalar=qkg[:, l, gidx:gidx + 1], in1=lnt[rb][:],
                                op0=ALU.mult, op1=ALU.mult),
                                reads=[("raw", rb), ("lnt", rb), "qkg"], writes=[("stg", slot, g)])
                    Sd.dma("sp", lambda e, ci=ci, stg=stg: e.dma_start(out=projT[ci * 128:(ci + 1) * 128, :], in_=stg[:]),
                           reads=[("stg", slot, g) for g in range(4)])
                Sd.barrier()
                es1.close()
                wv512 = sbt(es, "wv512", [128, KC, 512], BF16)
                vst = [sbt(es, f"vst{i}", [128, 512], BF16) for i in range(2)]
                vblocks = [(640, 128, 0), (1792, 512, 128), (4352, 512, 640), (4864, 512, 1152)]
                vi = 0
                for (col0, ncols, vc0) in vblocks:
                    wvv = w_in[l].rearrange("(c p) m -> p c m", p=128)
                    Sd.dma("pool", lambda e, col0=col0, ncols=ncols, wvv=wvv: e.dma_start(
                        out=wv512[:, :, 0:ncols], in_=wvv[:, :, col0:col0 + ncols]), writes=["wv512"])
                    for tc_ in range(S // 128):
                        pb = vi % 2
                        sl = vi % 2
                        vi += 1
                        for k in range(KC):
                            Sd.op("pe", lambda e, k=k, pb=pb, tc_=tc_, ncols=ncols: e.matmul(
                                ps[:, pb, 0:ncols], lhsT=xb[:, k, tc_ * 128:(tc_ + 1) * 128], rhs=wv512[:, k, 0:ncols],
                                start=(k == 0), stop=(k == KC - 1)),
                                reads=["wv512", ("xb", tc_ // 2)], writes=[("ps", pb)])
                        Sd.op("dve", lambda e, pb=pb, sl=sl, tc_=tc_, ncols=ncols: e.tensor_scalar(
                            out=vst[sl][:, 0:ncols], in0=ps[:, pb, 0:ncols], scalar1=rstd_col[:, tc_:tc_ + 1],
                            scalar2=None, op0=ALU.mult),
                            reads=[("ps", pb), ("rstd_col", tc_ // 2)], writes=[("vst", sl)])
                        Sd.dma("sp", lambda e, sl=sl, tc_=tc_, ncols=ncols, vc0=vc0: e.dma_start(
                            out=vtok[tc_ * 128:(tc_ + 1) * 128, vc0:vc0 + ncols], in_=vst[sl][:, 0:ncols]),
                            reads=[("vst", sl)])
                Sd.barrier()

        def make_E(E2, idx, bias_ap, mask_t, tmp, tslot):
            Sd.dma("sp", lambda e: e.dma_start(out=tmp[tslot][:], in_=bias_ap), writes=[("etmp", tslot)])
            Sd.op("act", lambda e: e.activation(out=tmp[tslot][:], in_=tmp[tslot][:], func=AF.Exp),
                  reads=[("etmp", tslot)], writes=[("etmp", tslot)])
            tv = tmp[tslot][:].rearrange("p (a b) -> p a b", a=2)
            mv = mask_t[:].rearrange("p (a b) -> p a b", a=2)
            for dup in range(2):
                Sd.op("dve", lambda e, dup=dup: e.tensor_tensor(out=E2[:, idx, :, dup, :], in0=tv, in1=mv, op=ALU.mult),
                      reads=[("etmp", tslot), "mask"], writes=[("E2", idx)])

        def banded(qT, kT, Vget, E2, eidx, dil, acc, first, et, Pt, ctr):
            nb = S // dil // 128
            qv = qT[:].rearrange("p (n r) -> p n r", r=dil)
            kv = kT[:].rearrange("p (n r) -> p n r", r=dil)
            accv = acc[:].rearrange("p a (n r) -> p a n r", r=dil)
            Ev = E2[:, eidx].rearrange("p a b c -> p (a b c)")
            onesl = cmat[:, ONES, 0:64]

            def blk(b):
                return slice(128 * b, 128 * b + 128)

            for r in range(dil):
                for bp in range(nb // 2):
                    b0, b1 = 2 * bp, 2 * bp + 1
                    i = ctr[0]
                    ctr[0] += 1
                    sbk = i % 2
                    obk = 2 + i % 2
                    bf = i % 2
                    mms = []
                    if b0 > 0:
                        mms.append((0, b0 - 1, b0))
                    mms += [(128, b0, b1), (256, b0, b0), (384, b1, b1)]
                    for (c0, kb, qb) in mms:
                        Sd.op("pe", lambda e, c0=c0, kb=kb, qb=qb, sbk=sbk, r=r: e.matmul(
                            ps[:, sbk, c0:c0 + 128], lhsT=kv[:, blk(kb), r], rhs=qv[:, blk(qb), r], start=True, stop=True),
                            reads=["qT", "kT"], writes=[("ps", sbk)])
                    Sd.op("act", lambda e, sbk=sbk, bf=bf: e.activation(out=et[bf][:], in_=ps[:, sbk, :], func=AF.Exp),
                          reads=[("ps", sbk)], writes=[("et", bf)])
                    Sd.op("dve", lambda e, bf=bf: e.tensor_tensor(out=Pt[bf][:], in0=et[bf][:], in1=Ev, op=ALU.mult),
                          reads=[("et", bf), ("E2", eidx)], writes=[("Pt", bf)])
                    for (half, lhs_of) in ((0, None), (256, onesl)):
                        for (qi, pairs) in ((0, [(0, b0 - 1), (256, b0)]), (1, [(128, b0), (384, b1)])):
                            pl = [(pc, kb) for (pc, kb) in pairs if kb >= 0]
                            for n_, (pc, kb) in enumerate(pl):
                                lh = lhs_of if lhs_of is not None else Vget(r, kb)
                                Sd.op("pe", lambda e, lh=lh, pc=pc, obk=obk, half=half, qi=qi, bf=bf, n_=n_, tot=len(pl): e.matmul(
                                    ps[0:64, obk, half + qi * 128:half + qi * 128 + 128], lhsT=lh, rhs=Pt[bf][:, pc:pc + 128],
                                    start=(n_ == 0), stop=(n_ == tot - 1)),
                                    reads=[("Pt", bf), "V", "cmat"], writes=[("ps", obk)])
                    ov = ps[0:64, obk, :].rearrange("p (a b) -> p a b", a=2)
                    av = accv[:, :, 256 * bp:256 * bp + 256, r]
                    lo = 256 * bp * dil
                    akeys = [("acc", q) for q in range(lo // 1024, min(4, (lo + 256 * dil + 1023) // 1024))]
                    if first:
                        Sd.op("act", lambda e, ov=ov, av=av: e.activation(out=av, in_=ov, func=AF.Copy),
                              reads=[("ps", obk)], writes=akeys)
                    else:
                        Sd.op("dve", lambda e, ov=ov, av=av: e.tensor_tensor(out=av, in0=av, in1=ov, op=ALU.add),
                              reads=[("ps", obk)] + akeys, writes=akeys)

        def finish_head(acc, outT, row0, sink_ap, rec):
            AK = [("acc", q) for q in range(4)]
            if sink_ap is not None:
                Sd.op("dve", lambda e: e.tensor_scalar(out=acc[:, 1, :], in0=acc[:, 1, :], scalar1=sink_ap, scalar2=None,
                                                       op0=ALU.add), reads=AK + ["esink"], writes=AK)
            Sd.op("dve", lambda e: e.reciprocal(out=rec[:], in_=acc[:, 1, :]), reads=AK, writes=["rec"])
            Sd.op("dve", lambda e: e.tensor_tensor(out=outT[:], in0=acc[:, 0, :], in1=rec[:], op=ALU.mult),
                  reads=AK + ["rec"], writes=["outT"])
            Sd.dma("sp", lambda e: e.dma_start(out=mixT[row0:row0 + 64, :], in_=outT[:]), reads=["outT"])

        def load_qk(qT, kT, qrow, krow):
            Sd.dma("sp", lambda e: e.dma_start(out=qT[:], in_=projT[qrow:qrow + 64, :]), writes=["qT"])
            if krow is not None:
                Sd.dma("sp", lambda e: e.dma_start(out=kT[:], in_=projT[krow:krow + 64, :]), writes=["kT"])

        def phase_attn(l):
            with ExitStack() as es:
                qT = sbt(es, "qT", [64, S], BF16)
                kT = sbt(es, "kT", [64, S], BF16)
                acc = sbt(es, "acc", [64, 2, S], F32)
                rec = sbt(es, "rec", [64, S], F32)
                outT = sbt(es, "outT", [64, S], BF16)
                et = [sbt(es, f"et{i}", [128, 512], F32) for i in range(2)]
                Pt = [sbt(es, f"Pt{i}", [128, 512], BF16) for i in range(2)]
                etmp = [sbt(es, f"etmp{i}", [128, 256], F32) for i in range(2)]
                maskA = sbt(es, "maskA", [128, 256], F32)
                maskC = sbt(es, "maskC", [128, 256], F32)
                E2 = sbt(es, "E2", [128, 12, 2, 2, 128], F32)
                Vd = [sbt(es, f"Vd{i}", [128, 32, 256], BF16) for i in range(3)]
                ctr = [0]
                Sd.dma("sp", lambda e: e.dma_start(out=maskA[:], in_=maskA_d), writes=["mask"])
                Sd.dma("sp", lambda e: e.dma_start(out=maskC[:], in_=maskC_d), writes=["mask"])
                for h in range(8):
                    make_E(E2, h, biasA_d[h], maskA, etmp, h % 2)
                Sd.dma("sp", lambda e: e.dma_start(
                    out=Vd[0][:, :, 0:128], in_=vtok.rearrange("(b j) c -> j b c", j=128)[:, :, 0:128]), writes=["V"])
                for h in range(8):
                    g = h // 4
                    load_qk(qT, kT, h * 64, (512 + g * 64) if h % 4 == 0 else None)
                    Vget = lambda r, kb, g=g: Vd[0][:, kb, g * 64:(g + 1) * 64]
                    banded(qT, kT, Vget, E2, h, 1, acc, True, et, Pt, ctr)
                    finish_head(acc, outT, h * 64, esink[0:64, l, h:h + 1], rec)
                for hg in range(4):
                    for di, dil in enumerate(DILS):
                        nb = S // dil // 128
                        vsrc = vtok.rearrange("(b j r) c -> j r b c", j=128, r=dil)
                        for r in range(dil):
                            Sd.dma("sp", lambda e, di=di, r=r, nb=nb, hg=hg, vsrc=vsrc: e.dma_start(
                                out=Vd[di][:, r * nb:(r + 1) * nb, :],
                                in_=vsrc[:, r, :, 640 + hg * 256:640 + (hg + 1) * 256]), writes=["V"])
                        for hh in range(4):
                            make_E(E2, di * 4 + hh, biasC_d[di, hg * 4 + hh], maskC, etmp, (di * 4 + hh) % 2)
                    for hh in range(4):
                        h = hg * 4 + hh
                        load_qk(qT, kT, 13 * 128 + h * 64, 21 * 128 + h * 64)
                        for di, dil in enumerate(DILS):
                            nb = S // dil // 128
                            Vget = lambda r, kb, di=di, nb=nb, hh=hh: Vd[di][:, r * nb + kb, hh * 64:(hh + 1) * 64]
                            banded(qT, kT, Vget, E2, di * 4 + hh, dil, acc, di == 0, et, Pt, ctr)
                        finish_head(acc, outT, 1024 + h * 64, None, rec)
                Sd.barrier()
            with ExitStack() as es:
                NB = 3
                qT = sbt(es, "qTb", [64, S], BF16)
                kT = sbt(es, "kTb", [64, S], BF16)
                VB = sbt(es, "VB", [128, 32, 512], BF16)
                m01f = sbt(es, "m01f", [128, 4, 512], F32)
                m01 = sbt(es, "m01", [128, 4, 512], BF16)
                ee = [sbt(es, f"ee{i}", [128, 512], F32) for i in range(NB)]
                sp_ = [sbt(es, f"sp{i}", [128, 512], BF16) for i in range(NB)]
                tx = [sbt(es, f"tx{i}", [128, 512], F32) for i in range(NB)]
                At = [sbt(es, f"At{i}", [128, 512], BF16) for i in range(NB)]
                Tb = [sbt(es, f"Tb{i}", [128, 512], BF16) for i in range(2)]
                outB = [sbt(es, f"outB{i}", [64, 512], BF16) for i in range(2)]
                Sd.dma("sp", lambda e: e.dma_start(out=m01f[:].rearrange("p a b -> p (a b)"), in_=m01_d), writes=["m01f"])
                Sd.op("dve", lambda e: e.tensor_copy(out=m01[:], in_=m01f[:]), reads=["m01f"], writes=["m01"])
                Sd.dma("sp", lambda e: e.dma_start(
                    out=VB[:], in_=vtok.rearrange("(b j) c -> j b c", j=128)[:, :, 128:640]), writes=["VB"])
                it = 0
                oi = 0
                ti = 0
                for h in range(8):
                    Sd.dma("sp", lambda e, h=h: e.dma_start(out=qT[:], in_=projT[5 * 128 + h * 64:5 * 128 + h * 64 + 64, :]),
                           writes=["qT"])
                    Sd.dma("sp", lambda e, h=h: e.dma_start(out=kT[:], in_=projT[9 * 128 + h * 64:9 * 128 + h * 64 + 64, :]),
                           writes=["kT"])
                    for qt in range(S // 512):
                        qs = slice(qt * 512, (qt + 1) * 512)
                        ob = 6 + oi % 2
                        osl = oi % 2
                        oi += 1
                        kb_top = 4 * qt + 3
                        for kb in range(kb_top, -1, -1):
                            c = kb - 4 * qt
                            b = it % NB
                            it += 1
                            zb = b
                            cb = 3 + b
                            ks = slice(kb * 128, kb * 128 + 128)
                            firstk = (kb == kb_top)
                            Sd.op("pe", lambda e, ks=ks, qs=qs, zb=zb: e.matmul(
                                ps[:, zb, :], lhsT=kT[:, ks], rhs=qT[:, qs], start=True, stop=True),
                                reads=["qT", "kT"], writes=[("ps", zb)])
                            Sd.op("act", lambda e, zb=zb, b=b: e.activation(out=ee[b][:], in_=ps[:, zb, :], func=AF.Exp),
                                  reads=[("ps", zb)], writes=[("ee", b)])
                            Sd.op("act", lambda e, b=b: e.activation(out=sp_[b][:], in_=ee[b][:], func=AF.Ln, bias=1.0),
                                  reads=[("ee", b)], writes=[("sp", b)])
                            if c >= 0:
                                Sd.op("dve", lambda e, b=b, c=c: e.tensor_tensor(
                                    out=sp_[b][:], in0=sp_[b][:], in1=m01[:, c, :], op=ALU.mult),
                                    reads=[("sp", b), "m01"], writes=[("sp", b)])
                            Sd.op("pe", lambda e, b=b, cb=cb, firstk=firstk: e.matmul(
                                ps[:, cb, :], lhsT=cmat[:, LTRI, :], rhs=sp_[b][:], start=True, stop=firstk),
                                reads=[("sp", b), "cmat"], writes=[("ps", cb)])
                            if not firstk:
                                tsl = (ti - 1) % 2
                                Sd.op("pe", lambda e, cb=cb, tsl=tsl: e.matmul(
                                    ps[:, cb, :], lhsT=cmat[64:65, ONES, :], rhs=Tb[tsl][64:65, :], start=False, stop=True),
                                    reads=[("Tb", tsl), "cmat"], writes=[("ps", cb)])
                            if kb > 0:
                                tsl = ti % 2
                                ti += 1
                                Sd.op("pe", lambda e, b=b, ob=ob, firstk=firstk: e.matmul(
                                    ps[64:65, ob, :], lhsT=cmat[:, ONES, 0:1], rhs=sp_[b][:], start=firstk, stop=(kb == 1)),
                                    reads=[("sp", b), "cmat"], writes=[("psT", ob)])
                                Sd.op("dve", lambda e, ob=ob, tsl=tsl: e.tensor_copy(out=Tb[tsl][64:65, :], in_=ps[64:65, ob, :]),
                                      reads=[("psT", ob)], writes=[("Tb", tsl)])
                            Sd.op("act", lambda e, b=b, cb=cb: e.activation(
                                out=tx[b][:], in_=ps[:, cb, :], func=AF.Exp, scale=-1.0),
                                reads=[("ps", cb)], writes=[("tx", b)])
                            Sd.op("dve", lambda e, b=b: e.tensor_tensor(
                                out=At[b][:], in0=ee[b][:], in1=tx[b][:], op=ALU.mult),
                                reads=[("ee", b), ("tx", b)], writes=[("At", b)])
                            if c >= 0:
                                Sd.op("dve", lambda e, b=b, c=c: e.tensor_tensor(
                                    out=At[b][:], in0=At[b][:], in1=m01[:, c, :], op=ALU.mult),
                                    reads=[("At", b), "m01"], writes=[("At", b)])
                            Sd.op("pe", lambda e, b=b, ob=ob, kb=kb, h=h, firstk=firstk: e.matmul(
                                ps[0:64, ob, :], lhsT=VB[:, kb, h * 64:(h + 1) * 64], rhs=At[b][:],
                                start=firstk, stop=(kb == 0)),
                                reads=[("At", b), "VB"], writes=[("ps", ob)])
                        Sd.op("dve", lambda e, ob=ob, osl=osl: e.tensor_copy(out=outB[osl][:], in_=ps[0:64, ob, :]),
                              reads=[("ps", ob)], writes=[("outB", osl)])
                        Sd.dma("sp", lambda e, osl=osl, h=h, qs=qs: e.dma_start(
                            out=mixT[512 + h * 64:512 + h * 64 + 64, qs], in_=outB[osl][:]), reads=[("outB", osl)])
                Sd.barrier()

        def phase_outproj(l, x_src, x_dst):
            with ExitStack() as es:
                mixn = sbt(es, "mixn", [128, KC, S], BF16)
                with ExitStack() as loc:
                    MT = 256
                    mt = [sbt(loc, f"mt{i}", [128, KC, MT], BF16) for i in range(2)]
                    sq = [sbt(loc, f"msq{i}", [128, KC, MT], BF16) for i in range(2)]
                    rg = [sbt(loc, f"rg{i}", [128, 3, MT], F32) for i in range(2)]
                    mv = mixT.rearrange("(c p) t -> p c t", p=128)
                    groups = [(0, 4), (4, 8), (8, 16)]
                    for it in range(S // MT):
                        b = it % 2
                        t0 = it * MT
                        Sd.dma("sp", lambda e, b=b, t0=t0: e.dma_start(out=mt[b][:], in_=mv[:, :, t0:t0 + MT]),
                               writes=[("mt", b)])
                        Sd.op("act", lambda e, b=b: e.activation(out=sq[b][:], in_=mt[b][:], func=AF.Square),
                              reads=[("mt", b)], writes=[("msq", b)])
                        for gi, (c0, c1) in enumerate(groups):
                            pb = 3 * b + gi
                            for c in range(c0, c1):
                                Sd.op("pe", lambda e, b=b, c=c, pb=pb, c0=c0, c1=c1: e.matmul(
                                    ps[:, pb, 0:MT], lhsT=cmat[:, ONES, :], rhs=sq[b][:, c, :], start=(c == c0), stop=(c == c1 - 1)),
                                    reads=[("msq", b), "cmat"], writes=[("ps", pb)])
                            Sd.op("act", lambda e, b=b, gi=gi, pb=pb, c0=c0, c1=c1: e.activation(
                                out=rg[b][:, gi, :], in_=ps[:, pb, 0:MT], func=AF.Ln, bias=eps1[:, 0:1],
                                scale=1.0 / ((c1 - c0) * 128)),
                                reads=[("ps", pb), "eps1"], writes=[("rg", b, gi)])
                            Sd.op("act", lambda e, b=b, gi=gi: e.activation(
                                out=rg[b][:, gi, :], in_=rg[b][:, gi, :], func=AF.Exp, scale=-0.5),
                                reads=[("rg", b, gi)], writes=[("rg", b, gi)])
                            for c in range(c0, c1):
                                Sd.op("dve", lambda e, b=b, c=c, gi=gi, t0=t0: e.scalar_tensor_tensor(
                                    out=mixn[:, c, t0:t0 + MT], in0=mt[b][:, c, :], scalar=gvec[:, l, 1, c:c + 1],
                                    in1=rg[b][:, gi, :], op0=ALU.mult, op1=ALU.mult),
                                    reads=[("mt", b), ("rg", b, gi), "gvec"], writes=[("mixn", it)])
                    Sd.barrier()
                wbuf = [sbt(es, f"wbo{i}", [128, KC, 128], BF16) for i in range(2)]
                xo = [sbt(es, f"xo{i}", [128, 2, 512], F32) for i in range(2)]
                load_w(wbuf, 0, w_out[l], 0, 128, KC, "wb")
                grp = 0
                for mc in range(KC):
                    slot = mc % 2
                    if mc + 1 < KC:
                        load_w(wbuf, 1 - slot, w_out[l], (mc + 1) * 128, 128, KC, "wb")
                    for g in range(S // 1024):
                        pb = 2 * (grp % 2)
                        xs_ = grp % 2
                        grp += 1
                        t0 = g * 1024
                        Sd.dma("sp", lambda e, xs_=xs_, mc=mc, t0=t0: e.dma_start(
                            out=xo[xs_][:].rearrange("p a b -> p (a b)"), in_=x_src[mc * 128:(mc + 1) * 128, t0:t0 + 1024]),
                            writes=[("xo", xs_)])
                        for k in range(KC):
                            for tt in range(2):
                                Sd.op("pe", lambda e, k=k, tt=tt, pb=pb, slot=slot, t0=t0: e.matmul(
                                    ps[:, pb + tt, :], lhsT=wbuf[slot][:, k, :], rhs=mixn[:, k, t0 + tt * 512:t0 + (tt + 1) * 512],
                                    start=(k == 0), stop=(k == KC - 1)),
                                    reads=[("wb", slot)], writes=[("ps", pb + tt)])
                        Sd.op("dve", lambda e, xs_=xs_, pb=pb: e.tensor_tensor(
                            out=xo[xs_][:], in0=ps[:, pb:pb + 2, :], in1=xo[xs_][:], op=ALU.add),
                            reads=[("ps", pb), ("ps", pb + 1), ("xo", xs_)], writes=[("xo", xs_)])
                        Sd.dma("sp", lambda e, xs_=xs_, mc=mc, t0=t0: e.dma_start(
                            out=x_dst[mc * 128:(mc + 1) * 128, t0:t0 + 1024], in_=xo[xs_][:].rearrange("p a b -> p (a b)")),
                            reads=[("xo", xs_)])
                Sd.barrier()

        def phase_ffn_up(l, x_src):
            with ExitStack() as es:
                xb, rstd_bc, _ = prologue(es, l, x_src, 2, False)
                wbuf = [sbt(es, f"wbu{i}", [128, KC, 128], BF16) for i in range(4)]
                U = [sbt(es, f"U{i}", [128, 1026], F32) for i in range(2)]
                cv = [sbt(es, f"cv{i}", [128, 1024], F32) for i in range(2)]
                sg = sbt(es, "sg", [128, 1024], F32)
                stage = [sbt(es, f"stgu{i}", [128, 1024], BF16) for i in range(2)]

                def load_j(j):
                    for part in range(2):
                        load_w(wbuf, (j % 2) * 2 + part, w_up[l], part * DFF + j * 128, 128, KC, "wb")

                load_j(0)
                grp = 0
                si = 0
                for j in range(FC):
                    if j + 1 < FC:
                        load_j(j + 1)
                    for part in range(2):
                        Sd.op("dve", lambda e, part=part: e.memset(U[part][:, 0:2], 0.0), writes=[("U", part)])
                    for g in range(S // 1024):
                        t0 = g * 1024
                        ss = si % 2
                        si += 1
                        for part in range(2):
                            slot = (j % 2) * 2 + part
                            pb = 2 * (grp % 2)
                            grp += 1
                            fcol = part * FC + j
                            for k in range(KC):
                                for tt in range(2):
                                    Sd.op("pe", lambda e, k=k, tt=tt, pb=pb, slot=slot, t0=t0: e.matmul(
                                        ps[:, pb + tt, :], lhsT=wbuf[slot][:, k, :], rhs=xb[:, k, t0 + tt * 512:t0 + (tt + 1) * 512],
                                        start=(k == 0), stop=(k == KC - 1)),
                                        reads=[("wb", slot)], writes=[("ps", pb + tt)])
                            Sd.op("dve", lambda e, part=part, pb=pb, t0=t0: e.tensor_tensor(
                                out=U[part][:, 2:1026].rearrange("p (a b) -> p a b", a=2), in0=ps[:, pb:pb + 2, :],
                                in1=rstd_bc[:, t0:t0 + 1024].rearrange("p (a b) -> p a b", a=2), op=ALU.mult),
                                reads=[("ps", pb), ("ps", pb + 1)], writes=[("U", part)])
                            Sd.op("act", lambda e, part=part, fcol=fcol: e.activation(
                                out=cv[part][:], in_=U[part][:, 2:1026], func=AF.Identity,
                                bias=convb[:, l, fcol:fcol + 1], scale=convw[:, l, 2, fcol:fcol + 1]),
                                reads=[("U", part)], writes=[("cv", part)])
                            Sd.op("dve", lambda e, part=part, fcol=fcol: e.scalar_tensor_tensor(
                                out=cv[part][:], in0=U[part][:, 1:1025], scalar=convw[:, l, 1, fcol:fcol + 1], in1=cv[part][:],
                                op0=ALU.mult, op1=ALU.add),
                                reads=[("U", part), ("cv", part)], writes=[("cv", part)])
                            Sd.op("dve", lambda e, part=part, fcol=fcol: e.scalar_tensor_tensor(
                                out=cv[part][:], in0=U[part][:, 0:1024], scalar=convw[:, l, 0, fcol:fcol + 1], in1=cv[part][:],
                                op0=ALU.mult, op1=ALU.add),
                                reads=[("U", part), ("cv", part)], writes=[("cv", part)])
                            Sd.op("dve", lambda e, part=part: e.tensor_copy(out=U[part][:, 0:2], in_=U[part][:, 1024:1026]),
                                  reads=[("U", part)], writes=[("U", part)])
                            if part == 0:
                                Sd.op("act", lambda e: e.activation(out=sg[:], in_=cv[0][:], func=AF.Silu),
                                      reads=[("cv", 0)], writes=["sg"])
                            else:
                                Sd.op("dve", lambda e, ss=ss: e.tensor_tensor(
                                    out=stage[ss][:], in0=sg[:], in1=cv[1][:], op=ALU.mult),
                                    reads=["sg", ("cv", 1)], writes=[("stgu", ss)])
                        Sd.dma("sp", lambda e, j=j, ss=ss, t0=t0: e.dma_start(
                            out=actT[j * 128:(j + 1) * 128, t0:t0 + 1024], in_=stage[ss][:]), reads=[("stgu", ss)])
                Sd.barrier()

        def phase_ffn_down(l, x_src, x_dst):
            with ExitStack() as es:
                at = sbt(es, "at", [128, FC, 1024], BF16)
                wbuf = [sbt(es, f"wbd{i}", [128, FC, 128], BF16) for i in range(2)]
                xo = [sbt(es, f"xod{i}", [128, 2, 512], F32) for i in range(2)]
                av = actT.rearrange("(c p) t -> p c t", p=128)
                wdv = w_down[l].rearrange("(c p) m -> p c m", p=128)
                n = 0
                def load_wd(slot, mc):
                    for qq in range(4):
                        Sd.dma("pool", lambda e, qq=qq: e.dma_start(
                            out=wbuf[slot][:, qq * 11:(qq + 1) * 11, :], in_=wdv[:, qq * 11:(qq + 1) * 11, mc * 128:(mc + 1) * 128]),
                            writes=[("wb", slot, qq)])
                load_wd(0, 0)
                for tg in range(S // 1024):
                    t0 = tg * 1024
                    for hc in range(4):
                        Sd.dma("sp", lambda e, hc=hc, t0=t0: e.dma_start(
                            out=at[:, hc * 11:(hc + 1) * 11, :], in_=av[:, hc * 11:(hc + 1) * 11, t0:t0 + 1024]),
                            writes=[("at", hc)])
                    for mc in range(KC):
                        slot = n % 2
                        pb = 2 * (n % 2)
                        n += 1
                        if not (tg == S // 1024 - 1 and mc == KC - 1):
                            nmc = (mc + 1) % KC
                            load_wd(1 - slot, nmc)
                        Sd.dma("sp", lambda e, slot=slot, mc=mc, t0=t0: e.dma_start(
                            out=xo[slot][:].rearrange("p a b -> p (a b)"), in_=x_src[mc * 128:(mc + 1) * 128, t0:t0 + 1024]),
                            writes=[("xo", slot)])
                        for k in range(FC):
                            for tt in range(2):
                                Sd.op("pe", lambda e, k=k, tt=tt, pb=pb, slot=slot: e.matmul(
                                    ps[:, pb + tt, :], lhsT=wbuf[slot][:, k, :], rhs=at[:, k, tt * 512:(tt + 1) * 512],
                                    start=(k == 0), stop=(k == FC - 1)),
                                    reads=[("wb", slot, k // 11), ("at", k // 11)], writes=[("ps", pb + tt)])
                        Sd.op("dve", lambda e, slot=slot, pb=pb: e.tensor_tensor(
                            out=xo[slot][:], in0=ps[:, pb:pb + 2, :], in1=xo[slot][:], op=ALU.add),
                            reads=[("ps", pb), ("ps", pb + 1), ("xo", slot)], writes=[("xo", slot)])
                        Sd.dma("sp", lambda e, slot=slot, mc=mc, t0=t0: e.dma_start(
                            out=x_dst[mc * 128:(mc + 1) * 128, t0:t0 + 1024], in_=xo[slot][:].rearrange("p a b -> p (a b)")),
                            reads=[("xo", slot)])
                Sd.barrier()

        for l in range(L):
            x0 = xT_in if l == 0 else xres
            x2 = out_T if (l == L - 1) else xres
            phase_inproj(l, x0)
            phase_attn(l)
            phase_outproj(l, x0, xmid)
            phase_ffn_up(l, xmid)
            phase_ffn_down(l, xmid, x2)
        Sd.barrier()
        with nc.Block() as block:
            Sd.emit(block)
    return nc, Sd


def _t5_bucket(dist):
    max_exact = N_BUCKETS // 2
    d = np.maximum(dist, 0)
    df = np.maximum(d, 1).astype(np.float32)
    large = max_exact + (np.log(df / np.float32(max_exact)) / np.float32(math.log(T5_MAX / max_exact))
                         * np.float32(N_BUCKETS - max_exact)).astype(np.int32)
    large = np.minimum(large, N_BUCKETS - 1)
    return np.where(d < max_exact, d, large)


def _host_consts(rel_bias_table):
    j = np.arange(128)[:, None]
    i = np.arange(128)[None, :]
    rel_prev = i + 128 - j
    rel_cur = i - j
    rel = np.stack([rel_prev, rel_cur], axis=1)
    tab = np.asarray(rel_bias_table, np.float32)

    def bias_for(dil, heads, maxd):
        valid = (rel >= 0) & (rel <= maxd)
        bk = _t5_bucket(np.clip(rel, 0, 255) * dil)
        out = np.zeros((len(heads), 128, 2, 128), np.float32)
        for n, h in enumerate(heads):
            out[n] = np.where(valid, tab[bk, h], np.float32(0.0))
        return out.reshape(len(heads), 128, 256), valid.astype(np.float32).reshape(128, 256)

    biasA, maskA = bias_for(1, list(range(8)), 127)
    bc = []
    for dil in DILS:
        b, maskC = bias_for(dil, list(range(8, 24)), 128)
        bc.append(b)
    biasC = np.stack(bc, 0)
    s_ = np.arange(128)[:, None]
    t_ = np.arange(512)[None, :]
    m01 = np.stack([((c * 128 + s_) < t_).astype(np.float32) for c in range(4)], axis=1).reshape(128, 4 * 512)
    ones = np.ones((128, 128), np.float32)
    hd = np.arange(128) // 64
    bd = (hd[:, None] == hd[None, :]).astype(np.float32)
    ltri = (np.arange(128)[:, None] >= np.arange(128)[None, :]).astype(np.float32)
    cmat = np.stack([ones, bd, bd / np.float32(64.0), ltri], axis=1).reshape(128, 4 * 128)
    return dict(biasA=biasA, biasC=biasC, maskA=maskA, maskC=maskC, m01=np.ascontiguousarray(m01),
                cmat=np.ascontiguousarray(cmat))


def _layer_small(attn_norm, mix_out_gain, ffn_norm, a_q_gain, a_k_gain, c_q_gain, c_k_gain, a_sinks, conv_w, conv_b):
    L = attn_norm.shape[0]
    g3 = np.stack([attn_norm, mix_out_gain, ffn_norm], axis=1)
    gvec = g3.reshape(L, 3, KC, 128).transpose(3, 0, 1, 2).reshape(128, L * 3 * KC)
    qk = np.stack([a_q_gain, a_k_gain, c_q_gain, c_k_gain], axis=1)
    qkg = np.concatenate([qk, qk], axis=2).transpose(2, 0, 1).reshape(128, L * 4)
    sinks = np.broadcast_to(a_sinks.reshape(1, L * 8), (128, L * 8))
    cw = conv_w.reshape(L, 3, 2 * FC, 128).transpose(3, 0, 1, 2).reshape(128, L * 3 * 2 * FC)
    cb = conv_b.reshape(L, 2 * FC, 128).transpose(2, 0, 1).reshape(128, L * 2 * FC)
    f = lambda a: np.ascontiguousarray(a, dtype=np.float32)
    return dict(gvec=f(gvec), qkg=f(qkg), sinks=f(sinks), convw=f(cw), convb=f(cb))


_NC_CACHE = {}


def _get_nc(L):
    if L not in _NC_CACHE:
        _NC_CACHE[L] = build_nc(L)[0]
    return _NC_CACHE[L]


LAYERS_PER_LAUNCH = 4


def kernel(x, attn_norm, w_in, a_q_gain, a_k_gain, a_sinks, c_q_gain, c_k_gain, rel_bias_table,
           mix_out_gain, w_out, ffn_norm, w_up, conv_w, conv_b, w_down):
    x = np.asarray(x, np.float32)
    B = x.shape[0]
    depth = w_in.shape[0]
    consts = _host_consts(rel_bias_table)
    xT = [np.ascontiguousarray(x[b].T) for b in range(B)]
    LPL = LAYERS_PER_LAUNCH
    nc = _get_nc(LPL)
    for l0 in range(0, depth, LPL):
        sl = slice(l0, l0 + LPL)
        small = _layer_small(*[np.asarray(a, np.float32)[sl] for a in
                               (attn_norm, mix_out_gain, ffn_norm, a_q_gain, a_k_gain, c_q_gain, c_k_gain,
                                a_sinks, conv_w, conv_b)])
        shared = dict(w_in=np.ascontiguousarray(np.asarray(w_in, np.float32)[sl]),
                      w_out=np.ascontiguousarray(np.asarray(w_out, np.float32)[sl]),
                      w_up=np.ascontiguousarray(np.asarray(w_up, np.float32)[sl]),
                      w_down=np.ascontiguousarray(np.asarray(w_down, np.float32)[sl]))
        shared.update(small)
        shared.update(consts)
        in_maps = [dict(shared, xT=xT[b]) for b in range(B)]
        res = run_bass_kernel_spmd(nc, in_maps, core_ids=list(range(B)))
        xT = [res.results[b]["outT"] for b in range(B)]
    out = np.stack([np.ascontiguousarray(xT[b].T) for b in range(B)], axis=0)
    return out.astype(np.float32)
```

```python
import math
from contextlib import ExitStack

import numpy as np
import concourse.bass as bass
import concourse.mybir as mybir
from concourse.bass_utils import run_bass_kernel_spmd

F32 = mybir.dt.float32
BF16 = mybir.dt.bfloat16
AF = mybir.ActivationFunctionType
ALU = mybir.AluOpType

D = 2048
S = 4096
KC = D // 128
HD = 64
IN_W = 5376
DFF = 5632
FC = DFF // 128
EPS = 1e-6
N_BUCKETS = 32
T5_MAX = 2048
DILS = (1, 4, 16)
NVCOL = 1664

ENGS = ["pe", "act", "dve", "pool", "sp"]
N_DMA_SEMS = 6


class Sched:
    def __init__(self, nc):
        self.nc = nc
        self.ops = {e: [] for e in ENGS}
        self.cnt = {}
        self.waited = {e: {} for e in ENGS}
        self.last_w = {}
        self.readers = {}
        self.sems = {}
        self.dma_rr = {"sp": 0, "pool": 0, "act": 0}
        self.n_instr = 0

    def sem_names(self):
        names = list(ENGS)
        for q in ("sp", "pool", "act"):
            for i in range(N_DMA_SEMS):
                names.append(f"dma_{q}_{i}")
        return names

    def set_sems(self, d):
        self.sems = d
        for k in d:
            self.cnt[k] = 0

    def _deps(self, reads, writes):
        deps = []
        for r in reads:
            w = self.last_w.get(r)
            if w is not None:
                deps.append(w)
        for w_ in writes:
            w = self.last_w.get(w_)
            if w is not None:
                deps.append(w)
            deps.extend(self.readers.get(w_, ()))
        return deps

    def _waits(self, eng, deps, skip_same):
        need = {}
        for (k, v) in deps:
            if skip_same and k == eng:
                continue
            if need.get(k, 0) < v:
                need[k] = v
        out = []
        wd = self.waited[eng]
        for k, v in need.items():
            if wd.get(k, 0) >= v:
                continue
            wd[k] = v
            out.append((k, v))
        return out

    def _record(self, tok, reads, writes):
        for r in reads:
            self.readers.setdefault(r, []).append(tok)
        for w in writes:
            self.last_w[w] = tok
            self.readers[w] = []

    def op(self, eng, fn, reads=(), writes=(), skip_same=None):
        if skip_same is None:
            skip_same = (eng == "pe")
        waits = self._waits(eng, self._deps(reads, writes), skip_same)
        self.cnt[eng] += 1
        tok = (eng, self.cnt[eng])
        self.ops[eng].append((waits, fn, eng, 1))
        self._record(tok, reads, writes)
        self.n_instr += 1
        return tok

    def dma(self, q, fn, reads=(), writes=()):
        i = self.dma_rr[q]
        self.dma_rr[q] = (i + 1) % N_DMA_SEMS
        sk = f"dma_{q}_{i}"
        deps = self._deps(reads, writes)
        if self.cnt[sk] > 0:
            deps.append((sk, self.cnt[sk]))
        waits = self._waits(q, deps, False)
        self.cnt[sk] += 16
        tok = (sk, self.cnt[sk])
        self.ops[q].append((waits, fn, sk, 16))
        self._record(tok, reads, writes)
        self.n_instr += 1
        return tok

    def barrier(self):
        for e in ENGS:
            deps = [(k, v) for k, v in self.cnt.items() if v > 0 and k != e]
            waits = self._waits(e, deps, False)
            if waits:
                self.ops[e].append((waits, None, None, 0))
        self.last_w = {}
        self.readers = {}

    def emit(self, block):
        sems = self.sems

        def run(engobj, lst):
            for (waits, fn, sk, amt) in lst:
                for (k, v) in waits:
                    engobj.wait_ge(sems[k], v)
                if fn is not None:
                    fn(engobj).then_inc(sems[sk], amt)

        @block.tensor
        def _(e):
            run(e, self.ops["pe"])

        @block.scalar
        def _(e):
            run(e, self.ops["act"])

        @block.vector
        def _(e):
            run(e, self.ops["dve"])

        @block.gpsimd
        def _(e):
            run(e, self.ops["pool"])

        @block.sync
        def _(e):
            run(e, self.ops["sp"])


def build_nc(L, last_is_output=True, debug=False):
    nc = bass.Bass("TRN2", target_bir_lowering=False)

    def din(name, shape, dt=F32):
        return nc.dram_tensor(name, list(shape), dt, kind="ExternalInput").ap()

    def dscr(name, shape, dt, out=False):
        if out:
            return nc.dram_tensor(name, list(shape), dt, kind="ExternalOutput").ap()
        return nc.dram_tensor(name, list(shape), dt).ap()

    xT_in = din("xT", [D, S])
    w_in = din("w_in", [L, D, IN_W])
    w_out = din("w_out", [L, D, D])
    w_up = din("w_up", [L, D, 2 * DFF])
    w_down = din("w_down", [L, DFF, D])
    gvec_d = din("gvec", [128, L * 3 * KC])
    qkg_d = din("qkg", [128, L * 4])
    sink_d = din("sinks", [128, L * 8])
    convw_d = din("convw", [128, L * 3 * 2 * FC])
    convb_d = din("convb", [128, L * 2 * FC])
    biasA_d = din("biasA", [8, 128, 256])
    biasC_d = din("biasC", [3, 16, 128, 256])
    maskA_d = din("maskA", [128, 256])
    maskC_d = din("maskC", [128, 256])
    m01_d = din("m01", [128, 4 * 512])
    cmat_d = din("cmat", [128, 4 * 128])

    out_T = dscr("outT", [D, S], F32, out=True)
    xmid = dscr("xmid", [D, S], F32, out=debug)
    xres = dscr("xres", [D, S], F32)
    projT = dscr("projT", [29 * 128, S], BF16, out=debug)
    vtok = dscr("vtok", [S, NVCOL], BF16, out=debug)
    mixT = dscr("mixT", [D, S], BF16, out=debug)
    actT = dscr("actT", [DFF, S], BF16, out=debug)

    Sd = Sched(nc)

    with ExitStack() as top:
        uid = [0]

        def sbt(es, name, shape, dt):
            uid[0] += 1
            return es.enter_context(nc.sbuf_tensor(f"{name}_u{uid[0]}", list(shape), dt))

        ps = top.enter_context(nc.psum_tensor("ps", [128, 8, 512], F32))
        sems = {n: top.enter_context(nc.semaphore(n)) for n in Sd.sem_names()}
        Sd.set_sems(sems)

        gvec = sbt(top, "gvec_s", [128, L, 3, KC], F32)
        qkg = sbt(top, "qkg_s", [128, L, 4], F32)
        esink = sbt(top, "esink_s", [128, L, 8], F32)
        convw = sbt(top, "convw_s", [128, L, 3, 2 * FC], F32)
        convb = sbt(top, "convb_s", [128, L, 2 * FC], F32)
        cmat_f = sbt(top, "cmat_f", [128, 4, 128], F32)
        cmat = sbt(top, "cmat_b", [128, 4, 128], BF16)
        eps1 = sbt(top, "eps1", [128, 1], F32)
        eps64 = sbt(top, "eps64", [128, 1], F32)
        ONES, BD1, BD64, LTRI = 0, 1, 2, 3

        Sd.dma("sp", lambda e: e.dma_start(out=gvec[:].rearrange("p a b c -> p (a b c)"), in_=gvec_d), writes=["gvec"])
        Sd.dma("sp", lambda e: e.dma_start(out=qkg[:].rearrange("p a b -> p (a b)"), in_=qkg_d), writes=["qkg"])
        Sd.dma("sp", lambda e: e.dma_start(out=esink[:].rearrange("p a b -> p (a b)"), in_=sink_d), writes=["esink"])
        Sd.dma("sp", lambda e: e.dma_start(out=convw[:].rearrange("p a b c -> p (a b c)"), in_=convw_d), writes=["convw"])
        Sd.dma("sp", lambda e: e.dma_start(out=convb[:].rearrange("p a b -> p (a b)"), in_=convb_d), writes=["convb"])
        Sd.dma("sp", lambda e: e.dma_start(out=cmat_f[:].rearrange("p a b -> p (a b)"), in_=cmat_d), writes=["cmat_f"])
        Sd.op("dve", lambda e: e.tensor_copy(out=cmat[:], in_=cmat_f[:]), reads=["cmat_f"], writes=["cmat"])
        Sd.op("dve", lambda e: e.memset(eps1[:], EPS), writes=["eps1"])
        Sd.op("dve", lambda e: e.memset(eps64[:], 64.0 * EPS), writes=["eps64"])
        Sd.op("act", lambda e: e.activation(out=esink[:], in_=esink[:], func=AF.Exp), reads=["esink"], writes=["esink"])
        Sd.barrier()

        def prologue(es, l, x_src, gi, want_col):
            xb = sbt(es, "xb", [128, KC, S], BF16)
            rstd_bc = sbt(es, "rstd_bc", [128, S], F32)
            rstd_col = sbt(es, "rstd_col", [128, S // 128], F32) if want_col else None
            TT = 256
            with ExitStack() as loc:
                xf = [sbt(loc, f"xf{i}", [128, KC, TT], F32) for i in range(2)]
                sq = [sbt(loc, f"sq{i}", [128, KC, TT], BF16) for i in range(2)]
                lt = [sbt(loc, f"lt{i}", [128, TT], F32) for i in range(2)]
                xsv = x_src.rearrange("(c p) t -> p c t", p=128)
                for it in range(S // TT):
                    b = it % 2
                    t0 = it * TT
                    Sd.dma("sp", lambda e, b=b, t0=t0: e.dma_start(out=xf[b][:], in_=xsv[:, :, t0:t0 + TT]),
                           writes=[("xf", b)])
                    Sd.op("act", lambda e, b=b: e.activation(out=sq[b][:], in_=xf[b][:], func=AF.Square),
                          reads=[("xf", b)], writes=[("sq", b)])
                    for c in range(KC):
                        Sd.op("dve", lambda e, b=b, c=c, t0=t0: e.tensor_scalar(
                            out=xb[:, c, t0:t0 + TT], in0=xf[b][:, c, :], scalar1=gvec[:, l, gi, c:c + 1],
                            scalar2=None, op0=ALU.mult), reads=[("xf", b), "gvec"], writes=[("xb", it)])
                    pb = b
                    for c in range(KC):
                        Sd.op("pe", lambda e, b=b, c=c, pb=pb: e.matmul(
                            ps[:, pb, 0:TT], lhsT=cmat[:, ONES, :], rhs=sq[b][:, c, :], start=(c == 0), stop=(c == KC - 1)),
                            reads=[("sq", b), "cmat"], writes=[("ps", pb)])
                    Sd.op("act", lambda e, b=b, pb=pb: e.activation(
                        out=lt[b][:], in_=ps[:, pb, 0:TT], func=AF.Ln, bias=eps1[:, 0:1], scale=1.0 / D),
                        reads=[("ps", pb), "eps1"], writes=[("lt", b)])
                    Sd.op("act", lambda e, b=b, t0=t0: e.activation(
                        out=rstd_bc[:, t0:t0 + TT], in_=lt[b][:], func=AF.Exp, scale=-0.5),
                        reads=[("lt", b)], writes=[("rstd_bc", it)])
                    if want_col:
                        pc = 2 + b
                        for j in range(TT // 128):
                            for c in range(KC):
                                Sd.op("pe", lambda e, b=b, c=c, j=j, pc=pc: e.matmul(
                                    ps[:, pc, j:j + 1], lhsT=sq[b][:, c, j * 128:(j + 1) * 128], rhs=cmat[:, ONES, 0:1],
                                    start=(c == 0), stop=(c == KC - 1)),
                                    reads=[("sq", b), "cmat"], writes=[("ps", pc)])
                        nj = TT // 128
                        c0 = it * nj
                        Sd.op("act", lambda e, b=b, pc=pc, nj=nj: e.activation(
                            out=lt[b][:, 0:nj], in_=ps[:, pc, 0:nj], func=AF.Ln, bias=eps1[:, 0:1], scale=1.0 / D),
                            reads=[("ps", pc), "eps1", ("lt", b)], writes=[("lt", b)])
                        Sd.op("act", lambda e, b=b, nj=nj, c0=c0: e.activation(
                            out=rstd_col[:, c0:c0 + nj], in_=lt[b][:, 0:nj], func=AF.Exp, scale=-0.5),
                            reads=[("lt", b)], writes=[("rstd_col", it)])
                Sd.barrier()
            return xb, rstd_bc, rstd_col

        XB_ALL = [("xb", it) for it in range(S // 256)]
        RB_ALL = [("rstd_bc", it) for it in range(S // 256)]

        def load_w(wbuf, slot, w_ap, col0, ncols, kc, key):
            wv = w_ap.rearrange("(c p) m -> p c m", p=128)
            Sd.dma("pool", lambda e: e.dma_start(out=wbuf[slot][:, 0:kc, 0:ncols], in_=wv[:, :, col0:col0 + ncols]),
                   writes=[(key, slot)])

        def phase_inproj(l, x_src):
            with ExitStack() as es:
                xb, rstd_bc, rstd_col = prologue(es, l, x_src, 0, True)
                es1 = ExitStack()
                wbuf = [sbt(es1, f"wb{i}", [128, KC, 128], BF16) for i in range(2)]
                stage = [sbt(es1, f"stg{i}", [128, S], BF16) for i in range(2)]
                raw = [sbt(es1, f"raw{i}", [128, 2, 512], F32) for i in range(2)]
                sqq = [sbt(es1, f"sqq{i}", [128, 2, 512], BF16) for i in range(2)]
                lnt = [sbt(es1, f"lnt{i}", [128, 2, 512], F32) for i in range(2)]
                chunks = []
                for i in range(4):
                    chunks.append((i * 128, "q", 0))
                chunks.append((512, "k", 1))
                for i in range(4):
                    chunks.append((768 + i * 128, "bq", None))
                for i in range(4):
                    chunks.append((1280 + i * 128, "p", None))
                for i in range(8):
                    chunks.append((2304 + i * 128, "q", 2))
                for i in range(8):
                    chunks.append((3328 + i * 128, "k", 3))
                assert len(chunks) == 29
                load_w(wbuf, 0, w_in[l], chunks[0][0], 128, KC, "wb")
                grp = 0
                pend_in = []
                for ci, (col0, kind, gidx) in enumerate(chunks):
                    slot = ci % 2
                    if ci + 1 < len(chunks):
                        load_w(wbuf, 1 - slot, w_in[l], chunks[ci + 1][0], 128, KC, "wb")
                    stg = stage[slot]
                    for g in range(S // 1024):
                        pb = 2 * (grp % 2)
                        rb = grp % 2
                        grp += 1
                        t0 = g * 1024
                        for k in range(KC):
                            for tt in range(2):
                                Sd.op("pe", lambda e, k=k, tt=tt, pb=pb, slot=slot, t0=t0: e.matmul(
                                    ps[:, pb + tt, :], lhsT=wbuf[slot][:, k, :], rhs=xb[:, k, t0 + tt * 512:t0 + (tt + 1) * 512],
                                    start=(k == 0), stop=(k == KC - 1)),
                                    reads=[("wb", slot)] + XB_ALL[4 * g:4 * g + 4], writes=[("ps", pb + tt)])
                        psv = ps[:, pb:pb + 2, :]
                        rbv = rstd_bc[:, t0:t0 + 1024].rearrange("p (a b) -> p a b", a=2)
                        stv = stg[:, t0:t0 + 1024].rearrange("p (a b) -> p a b", a=2)
                        pkeys = [("ps", pb), ("ps", pb + 1)]
                        if kind == "p":
                            Sd.op("dve", lambda e, psv=psv, rbv=rbv, stv=stv: e.tensor_tensor(
                                out=stv, in0=psv, in1=rbv, op=ALU.mult),
                                reads=pkeys + RB_ALL[4 * g:4 * g + 4], writes=[("stg", slot, g)])
                        elif kind == "bq":
                            Sd.op("dve", lambda e, psv=psv, rbv=rbv, stv=stv: e.scalar_tensor_tensor(
                                out=stv, in0=psv, scalar=0.125, in1=rbv, op0=ALU.mult, op1=ALU.mult),
                                reads=pkeys + RB_ALL[4 * g:4 * g + 4], writes=[("stg", slot, g)])
                        else:
                            Sd.op("dve", lambda e, psv=psv, rbv=rbv, rb=rb: e.tensor_tensor(
                                out=raw[rb][:], in0=psv, in1=rbv, op=ALU.mult),
                                reads=pkeys + RB_ALL[4 * g:4 * g + 4], writes=[("raw", rb)])
                            Sd.op("act", lambda e, rb=rb: e.activation(out=sqq[rb][:], in_=raw[rb][:], func=AF.Square),
                                  reads=[("raw", rb)], writes=[("sqq", rb)])
                            def tail(rb=rb, pb=pb, kind=kind, stv=stv, gidx=gidx, slot=slot, g=g):
                                mat = BD1 if kind == "q" else BD64
                                for tt in range(2):
                                    Sd.op("pe", lambda e, tt=tt: e.matmul(
                                        ps[:, 4 + pb + tt, :], lhsT=cmat[:, mat, :], rhs=sqq[rb][:, tt, :], start=True, stop=True),
                                        reads=[("sqq", rb), "cmat"], writes=[("ps", 4 + pb + tt)])
                                ept = eps64 if kind == "q" else eps1
                                Sd.op("act", lambda e: e.activation(
                                    out=lnt[rb][:], in_=ps[:, 4 + pb:6 + pb, :], func=AF.Ln, bias=ept[:, 0:1]),
                                    reads=[("ps", 4 + pb), ("ps", 5 + pb), "eps1", "eps64"], writes=[("lnt", rb)])
                                Sd.op("act", lambda e: e.activation(
                                    out=lnt[rb][:], in_=lnt[rb][:], func=AF.Exp, scale=-0.5),
                                    reads=[("lnt", rb)], writes=[("lnt", rb)])
                                Sd.op("dve", lambda e: e.scalar_tensor_tensor(
                                    out=stv, in0=raw[rb][:], scalar=qkg[:, l, gidx:gidx + 1], in1=lnt[rb][:],
                                    op0=ALU.mult, op1=ALU.mult),
                                    reads=[("raw", rb), ("lnt", rb), "qkg"], writes=[("stg", slot, g)])
                            pipe_push(pend_in, tail, 1)
                    pipe_push(pend_in, lambda ci=ci, stg=stg, slot=slot: Sd.dma(
                        "sp", lambda e: e.dma_start(out=projT[ci * 128:(ci + 1) * 128, :], in_=stg[:]),
                        reads=[("stg", slot, g) for g in range(4)]), 1)
                pipe_flush(pend_in)
                Sd.barrier()
                es1.close()
                wv512 = sbt(es, "wv512", [128, KC, 512], BF16)
                vst = [sbt(es, f"vst{i}", [128, 512], BF16) for i in range(2)]
                vblocks = [(640, 128, 0), (1792, 512, 128), (4352, 512, 640), (4864, 512, 1152)]
                vi = 0
                for (col0, ncols, vc0) in vblocks:
                    wvv = w_in[l].rearrange("(c p) m -> p c m", p=128)
                    Sd.dma("pool", lambda e, col0=col0, ncols=ncols, wvv=wvv: e.dma_start(
                        out=wv512[:, :, 0:ncols], in_=wvv[:, :, col0:col0 + ncols]), writes=["wv512"])
                    for tc_ in range(S // 128):
                        pb = vi % 2
                        sl = vi % 2
                        vi += 1
                        for k in range(KC):
                            Sd.op("pe", lambda e, k=k, pb=pb, tc_=tc_, ncols=ncols: e.matmul(
                                ps[:, pb, 0:ncols], lhsT=xb[:, k, tc_ * 128:(tc_ + 1) * 128], rhs=wv512[:, k, 0:ncols],
                                start=(k == 0), stop=(k == KC - 1)),
                                reads=["wv512", ("xb", tc_ // 2)], writes=[("ps", pb)])
                        Sd.op("dve", lambda e, pb=pb, sl=sl, tc_=tc_, ncols=ncols: e.tensor_scalar(
                            out=vst[sl][:, 0:ncols], in0=ps[:, pb, 0:ncols], scalar1=rstd_col[:, tc_:tc_ + 1],
                            scalar2=None, op0=ALU.mult),
                            reads=[("ps", pb), ("rstd_col", tc_ // 2)], writes=[("vst", sl)])
                        Sd.dma("sp", lambda e, sl=sl, tc_=tc_, ncols=ncols, vc0=vc0: e.dma_start(
                            out=vtok[tc_ * 128:(tc_ + 1) * 128, vc0:vc0 + ncols], in_=vst[sl][:, 0:ncols]),
                            reads=[("vst", sl)])
                Sd.barrier()

        def pipe_push(pend, fn, depth):
            pend.append(fn)
            while len(pend) > depth:
                pend.pop(0)()

        def pipe_flush(pend):
            while pend:
                pend.pop(0)()

        def make_E(E2, idx, bias_ap, mask_t, tmp, tslot):
            Sd.dma("sp", lambda e: e.dma_start(out=tmp[tslot][:], in_=bias_ap), writes=[("etmp", tslot)])
            Sd.op("act", lambda e: e.activation(out=tmp[tslot][:], in_=tmp[tslot][:], func=AF.Exp),
                  reads=[("etmp", tslot)], writes=[("etmp", tslot)])
            tv = tmp[tslot][:].rearrange("p (a b) -> p a b", a=2)
            mv = mask_t[:].rearrange("p (a b) -> p a b", a=2)
            for dup in range(2):
                Sd.op("dve", lambda e, dup=dup: e.tensor_tensor(out=E2[:, idx, dup, :, :], in0=tv, in1=mv, op=ALU.mult),
                      reads=[("etmp", tslot), "mask"], writes=[("E2", idx)])

        NBF = 3
        PDEPTH = 2

        def banded(qT, kT, hp, Vget, vkey, E2, eidx, dil, acc, ap_, first, et, Pt, ctr, pend):
            nb = S // dil // 128
            qv = qT[:].rearrange("p (n r) -> p n r", r=dil)
            kv = kT[:].rearrange("p (n r) -> p n r", r=dil)
            accv = acc[:].rearrange("p a (n r) -> p a n r", r=dil)
            Ev = E2[:, eidx].rearrange("p a b c -> p (a b c)")
            onesl = cmat[:, ONES, 0:64]

            def blk(b):
                return slice(128 * b, 128 * b + 128)

            for r in range(dil):
                for bp in range(nb // 2):
                    b0, b1 = 2 * bp, 2 * bp + 1
                    i = ctr[0]
                    ctr[0] += 1
                    sbk = i % NBF
                    obk = NBF + i % NBF
                    bf = i % NBF
                    mms = []
                    if b0 > 0:
                        mms.append((0, 128, b0 - 1, b0))
                    mms += [(128, 256, b0, b0), (384, 128, b1, b1)]
                    for (c0, nq, kb, qb) in mms:
                        Sd.op("pe", lambda e, c0=c0, nq=nq, kb=kb, qb=qb, sbk=sbk, r=r: e.matmul(
                            ps[:, sbk, c0:c0 + nq], lhsT=kv[:, blk(kb), r], rhs=qv[:, 128 * qb:128 * qb + nq, r],
                            start=True, stop=True),
                            reads=[("qT", hp), ("kT", hp)], writes=[("ps", sbk)])
                    Sd.op("act", lambda e, sbk=sbk, bf=bf: e.activation(out=et[bf][:], in_=ps[:, sbk, :], func=AF.Exp),
                          reads=[("ps", sbk)], writes=[("et", bf)])
                    Sd.op("dve", lambda e, bf=bf: e.tensor_tensor(out=Pt[bf][:], in0=et[bf][:], in1=Ev, op=ALU.mult),
                          reads=[("et", bf), ("E2", eidx)], writes=[("Pt", bf)])

                    def stage_b(r=r, bp=bp, b0=b0, b1=b1, obk=obk, bf=bf):
                        for (qi, pairs) in ((0, [(0, b0 - 1), (128, b0)]), (1, [(256, b0), (384, b1)])):
                            pl = [(pc, kb) for (pc, kb) in pairs if kb >= 0]
                            for n_, (pc, kb) in enumerate(pl):
                                Sd.op("pe", lambda e, lh=Vget(r, kb), pc=pc, qi=qi, n_=n_, tot=len(pl): e.matmul(
                                    ps[0:64, obk, qi * 128:qi * 128 + 128], lhsT=lh, rhs=Pt[bf][:, pc:pc + 128],
                                    start=(n_ == 0), stop=(n_ == tot - 1)),
                                    reads=[("Pt", bf), vkey], writes=[("ps", obk)])
                        if b0 > 0:
                            pv4 = Pt[bf][:].rearrange("p (d k i) -> p d k i", d=2, k=2)
                            dv = ps[0:64, obk, 256:512].rearrange("p (d i) -> p d i", d=2)
                            for kind_ in range(2):
                                Sd.op("pe", lambda e, kind_=kind_: e.matmul(
                                    dv, lhsT=onesl, rhs=pv4[:, :, kind_, :], start=(kind_ == 0), stop=(kind_ == 1)),
                                    reads=[("Pt", bf), "cmat"], writes=[("ps", obk)])
                        else:
                            for (qi, pcs) in ((0, [128]), (1, [256, 384])):
                                for n_, pc in enumerate(pcs):
                                    Sd.op("pe", lambda e, pc=pc, qi=qi, n_=n_, tot=len(pcs): e.matmul(
                                        ps[0:64, obk, 256 + qi * 128:256 + qi * 128 + 128], lhsT=onesl, rhs=Pt[bf][:, pc:pc + 128],
                                        start=(n_ == 0), stop=(n_ == tot - 1)),
                                        reads=[("Pt", bf), "cmat"], writes=[("ps", obk)])
                        ov = ps[0:64, obk, :].rearrange("p (a b) -> p a b", a=2)
                        av = accv[:, :, 256 * bp:256 * bp + 256, r]
                        lo = 256 * bp * dil
                        akeys = [("acc", ap_, q) for q in range(lo // 1024, min(4, (lo + 256 * dil + 1023) // 1024))]
                        if first:
                            Sd.op("act", lambda e: e.activation(out=av, in_=ov, func=AF.Copy),
                                  reads=[("ps", obk)], writes=akeys)
                        else:
                            Sd.op("dve", lambda e: e.tensor_tensor(out=av, in0=av, in1=ov, op=ALU.add),
                                  reads=[("ps", obk)] + akeys, writes=akeys)

                    pipe_push(pend, stage_b, PDEPTH)

        def finish_head(acc, ap_, outT, osl, row0, sink_ap):
            AK = [("acc", ap_, q) for q in range(4)]
            if sink_ap is not None:
                Sd.op("dve", lambda e: e.tensor_scalar(out=acc[:, 1, :], in0=acc[:, 1, :], scalar1=sink_ap, scalar2=None,
                                                       op0=ALU.add), reads=AK + ["esink"], writes=AK)
            Sd.op("act", lambda e: e.activation(out=acc[:, 1, :], in_=acc[:, 1, :], func=AF.Ln), reads=AK, writes=AK)
            Sd.op("act", lambda e: e.activation(out=acc[:, 1, :], in_=acc[:, 1, :], func=AF.Exp, scale=-1.0), reads=AK, writes=AK)
            Sd.op("pool", lambda e: e.tensor_tensor(out=outT[osl][:], in0=acc[:, 0, :], in1=acc[:, 1, :], op=ALU.mult),
                  reads=AK, writes=[("outT", osl)])
            Sd.dma("sp", lambda e: e.dma_start(out=mixT[row0:row0 + 64, :], in_=outT[osl][:]), reads=[("outT", osl)])

        def load_qk(qT, kT, hp, qrow, krow):
            Sd.dma("sp", lambda e: e.dma_start(out=qT[:], in_=projT[qrow:qrow + 64, :]), writes=[("qT", hp)])
            Sd.dma("sp", lambda e: e.dma_start(out=kT[:], in_=projT[krow:krow + 64, :]), writes=[("kT", hp)])

        def phase_attn(l):
            with ExitStack() as es:
                qTs = [sbt(es, f"qT{i}", [64, S], BF16) for i in range(2)]
                kTs = [sbt(es, f"kT{i}", [64, S], BF16) for i in range(2)]
                accs = [sbt(es, f"acc{i}", [64, 2, S], F32) for i in range(2)]
                outT = [sbt(es, f"outTs{i}", [64, S], BF16) for i in range(2)]
                et = [sbt(es, f"et{i}", [128, 512], BF16) for i in range(NBF)]
                Pt = [sbt(es, f"Pt{i}", [128, 512], BF16) for i in range(NBF)]
                etmp = [sbt(es, f"etmp{i}", [128, 256], F32) for i in range(2)]
                maskA = sbt(es, "maskA", [128, 256], F32)
                maskC = sbt(es, "maskC", [128, 256], F32)
                E2 = sbt(es, "E2", [128, 12, 2, 2, 128], BF16)
                Vd = [sbt(es, f"Vd{i}", [128, 32, 256], BF16) for i in range(3)]
                VA = sbt(es, "VA", [128, 32, 128], BF16)
                ctr = [0]
                pend = []
                Sd.dma("sp", lambda e: e.dma_start(out=maskA[:], in_=maskA_d), writes=["mask"])
                Sd.dma("sp", lambda e: e.dma_start(out=maskC[:], in_=maskC_d), writes=["mask"])
                heads = [("A", h) for h in range(8)] + [("C", h) for h in range(16)]

                def issue_loads(n):
                    kind, h = heads[n]
                    hp = n % 2
                    if kind == "A":
                        load_qk(qTs[hp], kTs[hp], hp, h * 64, 512 + (h // 4) * 64)
                    else:
                        load_qk(qTs[hp], kTs[hp], hp, 13 * 128 + h * 64, 21 * 128 + h * 64)

                for h in range(8):
                    make_E(E2, h, biasA_d[h], maskA, etmp, h % 2)
                Sd.dma("sp", lambda e: e.dma_start(
                    out=VA[:], in_=vtok.rearrange("(b j) c -> j b c", j=128)[:, :, 0:128]), writes=["VA"])
                issue_loads(0)
                for n, (kind, h) in enumerate(heads):
                    hp = n % 2
                    if n + 1 < len(heads):
                        issue_loads(n + 1)
                    acc = accs[hp]
                    if kind == "A":
                        g = h // 4
                        Vget = lambda r, kb, g=g: VA[:, kb, g * 64:(g + 1) * 64]
                        banded(qTs[hp], kTs[hp], hp, Vget, "VA", E2, h, 1, acc, hp, True, et, Pt, ctr, pend)
                        pipe_push(pend, lambda acc=acc, hp=hp, h=h: finish_head(
                            acc, hp, outT, hp, h * 64, esink[0:64, l, h:h + 1]), PDEPTH)
                    else:
                        hg, hh = h // 4, h % 4
                        if hh == 0:
                            pipe_flush(pend)
                            for di, dil in enumerate(DILS):
                                nb = S // dil // 128
                                vsrc = vtok.rearrange("(b j r) c -> j r b c", j=128, r=dil)
                                for r in range(dil):
                                    Sd.dma("sp", lambda e, di=di, r=r, nb=nb, hg=hg, vsrc=vsrc: e.dma_start(
                                        out=Vd[di][:, r * nb:(r + 1) * nb, :],
                                        in_=vsrc[:, r, :, 640 + hg * 256:640 + (hg + 1) * 256]), writes=[("Vd", di)])
                                for h2 in range(4):
                                    make_E(E2, di * 4 + h2, biasC_d[di, hg * 4 + h2], maskC, etmp, (di * 4 + h2) % 2)
                        for di, dil in enumerate(DILS):
                            nb = S // dil // 128
                            Vget = lambda r, kb, di=di, nb=nb, hh=hh: Vd[di][:, r * nb + kb, hh * 64:(hh + 1) * 64]
                            banded(qTs[hp], kTs[hp], hp, Vget, ("Vd", di), E2, di * 4 + hh, dil, acc, hp, di == 0,
                                   et, Pt, ctr, pend)
                        pipe_push(pend, lambda acc=acc, hp=hp, h=h: finish_head(
                            acc, hp, outT, hp, 1024 + h * 64, None), PDEPTH)
                pipe_flush(pend)
                Sd.barrier()
            with ExitStack() as es:
                NB = 3
                qTs = [sbt(es, f"qTb{i}", [64, S], BF16) for i in range(2)]
                kTs = [sbt(es, f"kTb{i}", [64, S], BF16) for i in range(2)]
                VB = sbt(es, "VB", [128, 32, 512], BF16)
                m01f = sbt(es, "m01f", [128, 4, 512], F32)
                m01 = sbt(es, "m01", [128, 4, 512], BF16)
                ee = [sbt(es, f"ee{i}", [128, 512], F32) for i in range(NB)]
                sp_ = [sbt(es, f"sp{i}", [128, 512], BF16) for i in range(NB)]
                tx = [sbt(es, f"tx{i}", [128, 512], F32) for i in range(NB)]
                At = [sbt(es, f"At{i}", [128, 512], BF16) for i in range(NB)]
                Tb = [sbt(es, f"Tb{i}", [128, 512], BF16) for i in range(2)]
                outB = [sbt(es, f"outB{i}", [64, 512], BF16) for i in range(2)]
                Sd.dma("sp", lambda e: e.dma_start(out=m01f[:].rearrange("p a b -> p (a b)"), in_=m01_d), writes=["m01f"])
                Sd.op("dve", lambda e: e.tensor_copy(out=m01[:], in_=m01f[:]), reads=["m01f"], writes=["m01"])
                Sd.dma("sp", lambda e: e.dma_start(
                    out=VB[:], in_=vtok.rearrange("(b j) c -> j b c", j=128)[:, :, 128:640]), writes=["VB"])

                def load_b(h):
                    hp = h % 2
                    Sd.dma("sp", lambda e: e.dma_start(out=qTs[hp][:], in_=projT[5 * 128 + h * 64:5 * 128 + h * 64 + 64, :]),
                           writes=[("qT", hp)])
                    Sd.dma("sp", lambda e: e.dma_start(out=kTs[hp][:], in_=projT[9 * 128 + h * 64:9 * 128 + h * 64 + 64, :]),
                           writes=[("kT", hp)])

                fifoB = []
                fifoC = []
                st = dict(it=0, oi=0, ti=0)
                load_b(0)
                for h in range(8):
                    hp = h % 2
                    if h + 1 < 8:
                        load_b(h + 1)
                    qT, kT = qTs[hp], kTs[hp]
                    for qt in range(S // 512):
                        qs = slice(qt * 512, (qt + 1) * 512)
                        ob = 6 + st["oi"] % 2
                        osl = st["oi"] % 2
                        st["oi"] += 1
                        kb_top = 4 * qt + 3
                        for kb in range(kb_top, -1, -1):
                            c = kb - 4 * qt
                            b = st["it"] % NB
                            st["it"] += 1
                            zb = b
                            cb = 3 + b
                            ks = slice(kb * 128, kb * 128 + 128)
                            firstk = (kb == kb_top)
                            Sd.op("pe", lambda e, ks=ks, qs=qs, zb=zb, qT=qT, kT=kT: e.matmul(
                                ps[:, zb, :], lhsT=kT[:, ks], rhs=qT[:, qs], start=True, stop=True),
                                reads=[("qT", hp), ("kT", hp)], writes=[("ps", zb)])
                            Sd.op("act", lambda e, zb=zb, b=b: e.activation(out=ee[b][:], in_=ps[:, zb, :], func=AF.Exp),
                                  reads=[("ps", zb)], writes=[("ee", b)])
                            Sd.op("act", lambda e, b=b: e.activation(out=sp_[b][:], in_=ee[b][:], func=AF.Ln, bias=1.0),
                                  reads=[("ee", b)], writes=[("sp", b)])
                            if c >= 0:
                                Sd.op("dve", lambda e, b=b, c=c: e.tensor_tensor(
                                    out=sp_[b][:], in0=sp_[b][:], in1=m01[:, c, :], op=ALU.mult),
                                    reads=[("sp", b), "m01"], writes=[("sp", b)])

                            def stage_c(b=b, ob=ob, kb=kb, h=h, firstk=firstk, osl=osl, qs=qs):
                                Sd.op("pe", lambda e: e.matmul(
                                    ps[0:64, ob, :], lhsT=VB[:, kb, h * 64:(h + 1) * 64], rhs=At[b][:],
                                    start=firstk, stop=(kb == 0)),
                                    reads=[("At", b), "VB"], writes=[("ps", ob)])
                                if kb == 0:
                                    Sd.op("dve", lambda e: e.tensor_copy(out=outB[osl][:], in_=ps[0:64, ob, :]),
                                          reads=[("ps", ob)], writes=[("outB", osl)])
                                    Sd.dma("sp", lambda e: e.dma_start(
                                        out=mixT[512 + h * 64:512 + h * 64 + 64, qs], in_=outB[osl][:]), reads=[("outB", osl)])

                            def stage_b(b=b, cb=cb, ob=ob, kb=kb, c=c, firstk=firstk, stage_c=stage_c):
                                Sd.op("pe", lambda e: e.matmul(
                                    ps[:, cb, :], lhsT=cmat[:, LTRI, :], rhs=sp_[b][:], start=True, stop=firstk),
                                    reads=[("sp", b), "cmat"], writes=[("ps", cb)])
                                if not firstk:
                                    tsl = (st["ti"] - 1) % 2
                                    Sd.op("pe", lambda e: e.matmul(
                                        ps[:, cb, :], lhsT=cmat[64:65, ONES, :], rhs=Tb[tsl][64:65, :], start=False, stop=True),
                                        reads=[("Tb", tsl), "cmat"], writes=[("ps", cb)])
                                if kb > 0:
                                    tsl2 = st["ti"] % 2
                                    st["ti"] += 1
                                    Sd.op("pe", lambda e: e.matmul(
                                        ps[64:65, ob, :], lhsT=cmat[:, ONES, 0:1], rhs=sp_[b][:], start=firstk, stop=(kb == 1)),
                                        reads=[("sp", b), "cmat"], writes=[("psT", ob)])
                                    Sd.op("dve", lambda e: e.tensor_copy(out=Tb[tsl2][64:65, :], in_=ps[64:65, ob, :]),
                                          reads=[("psT", ob)], writes=[("Tb", tsl2)])
                                Sd.op("act", lambda e: e.activation(
                                    out=tx[b][:], in_=ps[:, cb, :], func=AF.Exp, scale=-1.0),
                                    reads=[("ps", cb)], writes=[("tx", b)])
                                Sd.op("dve", lambda e: e.tensor_tensor(
                                    out=At[b][:], in0=ee[b][:], in1=tx[b][:], op=ALU.mult),
                                    reads=[("ee", b), ("tx", b)], writes=[("At", b)])
                                if c >= 0:
                                    Sd.op("dve", lambda e: e.tensor_tensor(
                                        out=At[b][:], in0=At[b][:], in1=m01[:, c, :], op=ALU.mult),
                                        reads=[("At", b), "m01"], writes=[("At", b)])
                                pipe_push(fifoC, stage_c, 1)

                            pipe_push(fifoB, stage_b, 1)
                pipe_flush(fifoB)
                pipe_flush(fifoC)
                Sd.barrier()

        def phase_outproj(l, x_src, x_dst):
            with ExitStack() as es:
                mixn = sbt(es, "mixn", [128, KC, S], BF16)
                with ExitStack() as loc:
                    MT = 256
                    mt = [sbt(loc, f"mt{i}", [128, KC, MT], BF16) for i in range(2)]
                    sq = [sbt(loc, f"msq{i}", [128, KC, MT], BF16) for i in range(2)]
                    rg = [sbt(loc, f"rg{i}", [128, 3, MT], F32) for i in range(2)]
                    mv = mixT.rearrange("(c p) t -> p c t", p=128)
                    groups = [(0, 4), (4, 8), (8, 16)]
                    for it in range(S // MT):
                        b = it % 2
                        t0 = it * MT
                        Sd.dma("sp", lambda e, b=b, t0=t0: e.dma_start(out=mt[b][:], in_=mv[:, :, t0:t0 + MT]),
                               writes=[("mt", b)])
                        Sd.op("act", lambda e, b=b: e.activation(out=sq[b][:], in_=mt[b][:], func=AF.Square),
                              reads=[("mt", b)], writes=[("msq", b)])
                        for gi, (c0, c1) in enumerate(groups):
                            pb = 3 * b + gi
                            for c in range(c0, c1):
                                Sd.op("pe", lambda e, b=b, c=c, pb=pb, c0=c0, c1=c1: e.matmul(
                                    ps[:, pb, 0:MT], lhsT=cmat[:, ONES, :], rhs=sq[b][:, c, :], start=(c == c0), stop=(c == c1 - 1)),
                                    reads=[("msq", b), "cmat"], writes=[("ps", pb)])
                            Sd.op("act", lambda e, b=b, gi=gi, pb=pb, c0=c0, c1=c1: e.activation(
                                out=rg[b][:, gi, :], in_=ps[:, pb, 0:MT], func=AF.Ln, bias=eps1[:, 0:1],
                                scale=1.0 / ((c1 - c0) * 128)),
                                reads=[("ps", pb), "eps1"], writes=[("rg", b, gi)])
                            Sd.op("act", lambda e, b=b, gi=gi: e.activation(
                                out=rg[b][:, gi, :], in_=rg[b][:, gi, :], func=AF.Exp, scale=-0.5),
                                reads=[("rg", b, gi)], writes=[("rg", b, gi)])
                            for c in range(c0, c1):
                                Sd.op("dve", lambda e, b=b, c=c, gi=gi, t0=t0: e.scalar_tensor_tensor(
                                    out=mixn[:, c, t0:t0 + MT], in0=mt[b][:, c, :], scalar=gvec[:, l, 1, c:c + 1],
                                    in1=rg[b][:, gi, :], op0=ALU.mult, op1=ALU.mult),
                                    reads=[("mt", b), ("rg", b, gi), "gvec"], writes=[("mixn", it)])
                    Sd.barrier()
                wbuf = [sbt(es, f"wbo{i}", [128, KC, 128], BF16) for i in range(2)]
                xo = [sbt(es, f"xo{i}", [128, 2, 512], F32) for i in range(2)]
                load_w(wbuf, 0, w_out[l], 0, 128, KC, "wb")
                grp = 0
                for mc in range(KC):
                    slot = mc % 2
                    if mc + 1 < KC:
                        load_w(wbuf, 1 - slot, w_out[l], (mc + 1) * 128, 128, KC, "wb")
                    for g in range(S // 1024):
                        pb = 2 * (grp % 2)
                        xs_ = grp % 2
                        grp += 1
                        t0 = g * 1024
                        Sd.dma("sp", lambda e, xs_=xs_, mc=mc, t0=t0: e.dma_start(
                            out=xo[xs_][:].rearrange("p a b -> p (a b)"), in_=x_src[mc * 128:(mc + 1) * 128, t0:t0 + 1024]),
                            writes=[("xo", xs_)])
                        for k in range(KC):
                            for tt in range(2):
                                Sd.op("pe", lambda e, k=k, tt=tt, pb=pb, slot=slot, t0=t0: e.matmul(
                                    ps[:, pb + tt, :], lhsT=wbuf[slot][:, k, :], rhs=mixn[:, k, t0 + tt * 512:t0 + (tt + 1) * 512],
                                    start=(k == 0), stop=(k == KC - 1)),
                                    reads=[("wb", slot)], writes=[("ps", pb + tt)])
                        Sd.op("dve", lambda e, xs_=xs_, pb=pb: e.tensor_tensor(
                            out=xo[xs_][:], in0=ps[:, pb:pb + 2, :], in1=xo[xs_][:], op=ALU.add),
                            reads=[("ps", pb), ("ps", pb + 1), ("xo", xs_)], writes=[("xo", xs_)])
                        Sd.dma("sp", lambda e, xs_=xs_, mc=mc, t0=t0: e.dma_start(
                            out=x_dst[mc * 128:(mc + 1) * 128, t0:t0 + 1024], in_=xo[xs_][:].rearrange("p a b -> p (a b)")),
                            reads=[("xo", xs_)])
                Sd.barrier()

        def phase_ffn_up(l, x_src):
            with ExitStack() as es:
                xb, rstd_bc, _ = prologue(es, l, x_src, 2, False)
                wbuf = [sbt(es, f"wbu{i}", [128, KC, 128], BF16) for i in range(4)]
                U = [sbt(es, f"U{i}", [128, 1026], F32) for i in range(2)]
                cv = [sbt(es, f"cv{i}", [128, 1024], F32) for i in range(2)]
                sg = sbt(es, "sg", [128, 1024], F32)
                stage = [sbt(es, f"stgu{i}", [128, 1024], BF16) for i in range(2)]

                def load_j(j):
                    for part in range(2):
                        load_w(wbuf, (j % 2) * 2 + part, w_up[l], part * DFF + j * 128, 128, KC, "wb")

                load_j(0)
                grp = 0
                si = 0
                for j in range(FC):
                    if j + 1 < FC:
                        load_j(j + 1)
                    for part in range(2):
                        Sd.op("dve", lambda e, part=part: e.memset(U[part][:, 0:2], 0.0), writes=[("U", part)])
                    for g in range(S // 1024):
                        t0 = g * 1024
                        ss = si % 2
                        si += 1
                        for part in range(2):
                            slot = (j % 2) * 2 + part
                            pb = 2 * (grp % 2)
                            grp += 1
                            fcol = part * FC + j
                            for k in range(KC):
                                for tt in range(2):
                                    Sd.op("pe", lambda e, k=k, tt=tt, pb=pb, slot=slot, t0=t0: e.matmul(
                                        ps[:, pb + tt, :], lhsT=wbuf[slot][:, k, :], rhs=xb[:, k, t0 + tt * 512:t0 + (tt + 1) * 512],
                                        start=(k == 0), stop=(k == KC - 1)),
                                        reads=[("wb", slot)], writes=[("ps", pb + tt)])
                            Sd.op("dve", lambda e, part=part, pb=pb, t0=t0: e.tensor_tensor(
                                out=U[part][:, 2:1026].rearrange("p (a b) -> p a b", a=2), in0=ps[:, pb:pb + 2, :],
                                in1=rstd_bc[:, t0:t0 + 1024].rearrange("p (a b) -> p a b", a=2), op=ALU.mult),
                                reads=[("ps", pb), ("ps", pb + 1)], writes=[("U", part)])
                            Sd.op("act", lambda e, part=part, fcol=fcol: e.activation(
                                out=cv[part][:], in_=U[part][:, 2:1026], func=AF.Identity,
                                bias=convb[:, l, fcol:fcol + 1], scale=convw[:, l, 2, fcol:fcol + 1]),
                                reads=[("U", part)], writes=[("cv", part)])
                            Sd.op("dve", lambda e, part=part, fcol=fcol: e.scalar_tensor_tensor(
                                out=cv[part][:], in0=U[part][:, 1:1025], scalar=convw[:, l, 1, fcol:fcol + 1], in1=cv[part][:],
                                op0=ALU.mult, op1=ALU.add),
                                reads=[("U", part), ("cv", part)], writes=[("cv", part)])
                            Sd.op("dve", lambda e, part=part, fcol=fcol: e.scalar_tensor_tensor(
                                out=cv[part][:], in0=U[part][:, 0:1024], scalar=convw[:, l, 0, fcol:fcol + 1], in1=cv[part][:],
                                op0=ALU.mult, op1=ALU.add),
                                reads=[("U", part), ("cv", part)], writes=[("cv", part)])
                            Sd.op("dve", lambda e, part=part: e.tensor_copy(out=U[part][:, 0:2], in_=U[part][:, 1024:1026]),
                                  reads=[("U", part)], writes=[("U", part)])
                            if part == 0:
                                Sd.op("act", lambda e: e.activation(out=sg[:], in_=cv[0][:], func=AF.Silu),
                                      reads=[("cv", 0)], writes=["sg"])
                            else:
                                Sd.op("dve", lambda e, ss=ss: e.tensor_tensor(
                                    out=stage[ss][:], in0=sg[:], in1=cv[1][:], op=ALU.mult),
                                    reads=["sg", ("cv", 1)], writes=[("stgu", ss)])
                        Sd.dma("sp", lambda e, j=j, ss=ss, t0=t0: e.dma_start(
                            out=actT[j * 128:(j + 1) * 128, t0:t0 + 1024], in_=stage[ss][:]), reads=[("stgu", ss)])
                Sd.barrier()

        def phase_ffn_down(l, x_src, x_dst):
            with ExitStack() as es:
                at = sbt(es, "at", [128, FC, 1024], BF16)
                wbuf = [sbt(es, f"wbd{i}", [128, FC, 128], BF16) for i in range(2)]
                xo = [sbt(es, f"xod{i}", [128, 2, 512], F32) for i in range(2)]
                av = actT.rearrange("(c p) t -> p c t", p=128)
                wdv = w_down[l].rearrange("(c p) m -> p c m", p=128)
                n = 0
                def load_wd(slot, mc):
                    for qq in range(4):
                        Sd.dma("pool", lambda e, qq=qq: e.dma_start(
                            out=wbuf[slot][:, qq * 11:(qq + 1) * 11, :], in_=wdv[:, qq * 11:(qq + 1) * 11, mc * 128:(mc + 1) * 128]),
                            writes=[("wb", slot, qq)])
                load_wd(0, 0)
                for tg in range(S // 1024):
                    t0 = tg * 1024
                    for hc in range(4):
                        Sd.dma("sp", lambda e, hc=hc, t0=t0: e.dma_start(
                            out=at[:, hc * 11:(hc + 1) * 11, :], in_=av[:, hc * 11:(hc + 1) * 11, t0:t0 + 1024]),
                            writes=[("at", hc)])
                    for mc in range(KC):
                        slot = n % 2
                        pb = 2 * (n % 2)
                        n += 1
                        if not (tg == S // 1024 - 1 and mc == KC - 1):
                            nmc = (mc + 1) % KC
                            load_wd(1 - slot, nmc)
                        Sd.dma("sp", lambda e, slot=slot, mc=mc, t0=t0: e.dma_start(
                            out=xo[slot][:].rearrange("p a b -> p (a b)"), in_=x_src[mc * 128:(mc + 1) * 128, t0:t0 + 1024]),
                            writes=[("xo", slot)])
                        for k in range(FC):
                            for tt in range(2):
                                Sd.op("pe", lambda e, k=k, tt=tt, pb=pb, slot=slot: e.matmul(
                                    ps[:, pb + tt, :], lhsT=wbuf[slot][:, k, :], rhs=at[:, k, tt * 512:(tt + 1) * 512],
                                    start=(k == 0), stop=(k == FC - 1)),
                                    reads=[("wb", slot, k // 11), ("at", k // 11)], writes=[("ps", pb + tt)])
                        Sd.op("dve", lambda e, slot=slot, pb=pb: e.tensor_tensor(
                            out=xo[slot][:], in0=ps[:, pb:pb + 2, :], in1=xo[slot][:], op=ALU.add),
                            reads=[("ps", pb), ("ps", pb + 1), ("xo", slot)], writes=[("xo", slot)])
                        Sd.dma("sp", lambda e, slot=slot, mc=mc, t0=t0: e.dma_start(
                            out=x_dst[mc * 128:(mc + 1) * 128, t0:t0 + 1024], in_=xo[slot][:].rearrange("p a b -> p (a b)")),
                            reads=[("xo", slot)])
                Sd.barrier()

        for l in range(L):
            x0 = xT_in if l == 0 else xres
            x2 = out_T if (l == L - 1) else xres
            phase_inproj(l, x0)
            phase_attn(l)
            phase_outproj(l, x0, xmid)
            phase_ffn_up(l, xmid)
            phase_ffn_down(l, xmid, x2)
        Sd.barrier()
        with nc.Block() as block:
            Sd.emit(block)
    return nc, Sd


def _t5_bucket(dist):
    max_exact = N_BUCKETS // 2
    d = np.maximum(dist, 0)
    df = np.maximum(d, 1).astype(np.float32)
    large = max_exact + (np.log(df / np.float32(max_exact)) / np.float32(math.log(T5_MAX / max_exact))
                         * np.float32(N_BUCKETS - max_exact)).astype(np.int32)
    large = np.minimum(large, N_BUCKETS - 1)
    return np.where(d < max_exact, d, large)


def _host_consts(rel_bias_table):
    j = np.arange(128)[:, None]
    i = np.arange(128)[None, :]
    rel_prev = i + 128 - j
    rel_cur = i - j
    rel = np.stack([rel_prev, rel_cur], axis=1)
    tab = np.asarray(rel_bias_table, np.float32)

    def bias_for(dil, heads, maxd):
        valid = (rel >= 0) & (rel <= maxd)
        bk = _t5_bucket(np.clip(rel, 0, 255) * dil)
        out = np.zeros((len(heads), 128, 2, 128), np.float32)
        for n, h in enumerate(heads):
            out[n] = np.where(valid, tab[bk, h], np.float32(0.0))
        return out.reshape(len(heads), 128, 256), valid.astype(np.float32).reshape(128, 256)

    biasA, maskA = bias_for(1, list(range(8)), 127)
    bc = []
    for dil in DILS:
        b, maskC = bias_for(dil, list(range(8, 24)), 128)
        bc.append(b)
    biasC = np.stack(bc, 0)
    s_ = np.arange(128)[:, None]
    t_ = np.arange(512)[None, :]
    m01 = np.stack([((c * 128 + s_) < t_).astype(np.float32) for c in range(4)], axis=1).reshape(128, 4 * 512)
    ones = np.ones((128, 128), np.float32)
    hd = np.arange(128) // 64
    bd = (hd[:, None] == hd[None, :]).astype(np.float32)
    ltri = (np.arange(128)[:, None] >= np.arange(128)[None, :]).astype(np.float32)
    cmat = np.stack([ones, bd, bd / np.float32(64.0), ltri], axis=1).reshape(128, 4 * 128)
    return dict(biasA=biasA, biasC=biasC, maskA=maskA, maskC=maskC, m01=np.ascontiguousarray(m01),
                cmat=np.ascontiguousarray(cmat))


def _layer_small(attn_norm, mix_out_gain, ffn_norm, a_q_gain, a_k_gain, c_q_gain, c_k_gain, a_sinks, conv_w, conv_b):
    L = attn_norm.shape[0]
    g3 = np.stack([attn_norm, mix_out_gain, ffn_norm], axis=1)
    gvec = g3.reshape(L, 3, KC, 128).transpose(3, 0, 1, 2).reshape(128, L * 3 * KC)
    qk = np.stack([a_q_gain, a_k_gain, c_q_gain, c_k_gain], axis=1)
    qkg = np.concatenate([qk, qk], axis=2).transpose(2, 0, 1).reshape(128, L * 4)
    sinks = np.broadcast_to(a_sinks.reshape(1, L * 8), (128, L * 8))
    cw = conv_w.reshape(L, 3, 2 * FC, 128).transpose(3, 0, 1, 2).reshape(128, L * 3 * 2 * FC)
    cb = conv_b.reshape(L, 2 * FC, 128).transpose(2, 0, 1).reshape(128, L * 2 * FC)
    f = lambda a: np.ascontiguousarray(a, dtype=np.float32)
    return dict(gvec=f(gvec), qkg=f(qkg), sinks=f(sinks), convw=f(cw), convb=f(cb))


_NC_CACHE = {}


def _get_nc(L):
    if L not in _NC_CACHE:
        _NC_CACHE[L] = build_nc(L)[0]
    return _NC_CACHE[L]


LAYERS_PER_LAUNCH = 4


def kernel(x, attn_norm, w_in, a_q_gain, a_k_gain, a_sinks, c_q_gain, c_k_gain, rel_bias_table,
           mix_out_gain, w_out, ffn_norm, w_up, conv_w, conv_b, w_down):
    x = np.asarray(x, np.float32)
    B = x.shape[0]
    depth = w_in.shape[0]
    consts = _host_consts(rel_bias_table)
    xT = [np.ascontiguousarray(x[b].T) for b in range(B)]
    LPL = LAYERS_PER_LAUNCH
    nc = _get_nc(LPL)
    for l0 in range(0, depth, LPL):
        sl = slice(l0, l0 + LPL)
        small = _layer_small(*[np.asarray(a, np.float32)[sl] for a in
                               (attn_norm, mix_out_gain, ffn_norm, a_q_gain, a_k_gain, c_q_gain, c_k_gain,
                                a_sinks, conv_w, conv_b)])
        shared = dict(w_in=np.ascontiguousarray(np.asarray(w_in, np.float32)[sl]),
                      w_out=np.ascontiguousarray(np.asarray(w_out, np.float32)[sl]),
                      w_up=np.ascontiguousarray(np.asarray(w_up, np.float32)[sl]),
                      w_down=np.ascontiguousarray(np.asarray(w_down, np.float32)[sl]))
        shared.update(small)
        shared.update(consts)
        in_maps = [dict(shared, xT=xT[b]) for b in range(B)]
        res = run_bass_kernel_spmd(nc, in_maps, core_ids=list(range(B)))
        xT = [res.results[b]["outT"] for b in range(B)]
    out = np.stack([np.ascontiguousarray(xT[b].T) for b in range(B)], axis=0)
    return out.astype(np.float32)
```
